# Optimizing a Trainium2 kernel written in Bass

```python
import jax, jax.numpy as jnp
from jax import lax
import numpy as np

D_MODEL = 2048
BATCH = 4
SEQ = 4096
DEPTH = 4

N_MIXERS = 3
EXPAND = 2
D_INNER = EXPAND * D_MODEL
CHUNK = 128
SGU_HEADS = 8
SGU_HEAD_DIM = D_INNER // SGU_HEADS
POOL_WINDOWS = (2, 4, 8, 16)
POOL_GROUPS = len(POOL_WINDOWS)
POOL_GROUP_DIM = D_INNER // POOL_GROUPS
CONV_WIDTH = 31
LN_EPS = 1e-5
ALPHA = (2.0 * DEPTH) ** 0.25
BETA = (8.0 * DEPTH) ** -0.25
N_A = (DEPTH + 2) // 3
N_B = (DEPTH + 1) // 3
N_C = DEPTH // 3

kernel_name = 'hybrid_sgu_pool_conformer_deepnorm'


def layer_norm(x, g, b):
    xf = x.astype(jnp.float32)
    mu = jnp.mean(xf, axis=-1, keepdims=True)
    var = jnp.mean(jnp.square(xf - mu), axis=-1, keepdims=True)
    y = (xf - mu) * lax.rsqrt(var + LN_EPS) * g.astype(jnp.float32) + b.astype(jnp.float32)
    return y.astype(x.dtype)


def mixer_sgu(h, w_in, ln_g, ln_b, w_s, b_s, w_out):
    bsz, seq, _ = h.shape
    proj = h @ w_in
    u, v, z = jnp.split(proj, 3, axis=-1)
    u = jax.nn.gelu(u)
    v = layer_norm(jax.nn.gelu(v), ln_g, ln_b)
    vc = v.reshape(bsz, seq // CHUNK, CHUNK, SGU_HEADS, SGU_HEAD_DIM)
    causal = jnp.tril(jnp.ones((CHUNK, CHUNK), dtype=w_s.dtype))
    ws = w_s * causal[None]
    mixed = jnp.einsum('hts,bcshe->bcthe', ws, vc) + b_s.T[:, :, None]
    mixed = mixed.reshape(bsz, seq, D_INNER)
    y = u * mixed * jax.nn.silu(z)
    return y @ w_out


def mixer_pool(h, w_in, w_pool, scale, w_out):
    bsz, seq, _ = h.shape
    proj = h @ w_in
    v, z = jnp.split(proj, 2, axis=-1)
    vf = v.astype(jnp.float32).reshape(bsz, seq, POOL_GROUPS, POOL_GROUP_DIM)
    cs = jnp.cumsum(vf, axis=1)
    pos = jnp.arange(seq)
    outs = []
    for g, w in enumerate(POOL_WINDOWS):
        c = cs[:, :, g]
        lag = jnp.pad(c, ((0, 0), (w, 0), (0, 0)))[:, :seq]
        cnt = jnp.minimum(pos + 1, w).astype(jnp.float32)
        mean = (c - lag) / cnt[None, :, None]
        outs.append(mean - vf[:, :, g])
    p = jnp.stack(outs, axis=2).astype(v.dtype)
    p = jnp.einsum('bsgc,gcd->bsgd', p, w_pool).reshape(bsz, seq, D_INNER) * scale
    return (p * jax.nn.silu(z)) @ w_out


def mixer_conv(h, w_in, conv_w, conv_b, ln_g, ln_b, w_out):
    proj = h @ w_in
    a, gl, z = jnp.split(proj, 3, axis=-1)
    g = a * jax.nn.sigmoid(gl)
    c = lax.conv_general_dilated(
        g, conv_w[:, None, :], window_strides=(1,), padding=[(CONV_WIDTH - 1, 0)],
        dimension_numbers=('NWC', 'WIO', 'NWC'), feature_group_count=D_INNER) + conv_b
    s = jax.nn.silu(layer_norm(c, ln_g, ln_b))
    return (s * jax.nn.silu(z)) @ w_out


def setup_inputs(seed: int = 0) -> dict:
    key = jax.random.key(seed)
    ks = jax.random.split(key, 24)
    f32 = jnp.float32
    nrm = lambda k, shape: jax.random.normal(k, shape, dtype=f32)
    E, D = D_INNER, D_MODEL
    out_scale = BETA * E ** -0.5
    return {
        'x': nrm(ks[0], (BATCH, SEQ, D)),
        'a_w_in': nrm(ks[1], (N_A, D, 3 * E)) * D ** -0.5,
        'a_ln_g': 1.0 + 0.02 * nrm(ks[2], (N_A, E)),
        'a_ln_b': 0.02 * nrm(ks[3], (N_A, E)),
        'a_w_s': nrm(ks[4], (N_A, SGU_HEADS, CHUNK, CHUNK)) * CHUNK ** -0.5,
        'a_b_s': 1.0 + 0.02 * nrm(ks[5], (N_A, SGU_HEADS, CHUNK)),
        'a_w_out': nrm(ks[6], (N_A, E, D)) * out_scale,
        'b_w_in': nrm(ks[7], (N_B, D, 2 * E)) * D ** -0.5,
        'b_w_pool': nrm(ks[8], (N_B, POOL_GROUPS, POOL_GROUP_DIM, POOL_GROUP_DIM)) * POOL_GROUP_DIM ** -0.5,
        'b_scale': 1.0 + 0.1 * nrm(ks[9], (N_B, E)),
        'b_w_out': nrm(ks[10], (N_B, E, D)) * out_scale,
        'c_w_in': nrm(ks[11], (N_C, D, 3 * E)) * D ** -0.5,
        'c_conv_w': nrm(ks[12], (N_C, CONV_WIDTH, E)) * CONV_WIDTH ** -0.5,
        'c_conv_b': 0.02 * nrm(ks[13], (N_C, E)),
        'c_ln_g': 1.0 + 0.02 * nrm(ks[14], (N_C, E)),
        'c_ln_b': 0.02 * nrm(ks[15], (N_C, E)),
        'c_w_out': nrm(ks[16], (N_C, E, D)) * out_scale,
        'post_ln_g': 1.0 + 0.02 * nrm(ks[17], (DEPTH, D)),
        'post_ln_b': 0.02 * nrm(ks[18], (DEPTH, D)),
    }


def reference(x, a_w_in, a_ln_g, a_ln_b, a_w_s, a_b_s, a_w_out,
              b_w_in, b_w_pool, b_scale, b_w_out,
              c_w_in, c_conv_w, c_conv_b, c_ln_g, c_ln_b, c_w_out,
              post_ln_g, post_ln_b):
    h = x
    for i in range(DEPTH):
        kind, j = i % N_MIXERS, i // N_MIXERS
        if kind == 0:
            y = mixer_sgu(h, a_w_in[j], a_ln_g[j], a_ln_b[j], a_w_s[j], a_b_s[j], a_w_out[j])
        elif kind == 1:
            y = mixer_pool(h, b_w_in[j], b_w_pool[j], b_scale[j], b_w_out[j])
        else:
            y = mixer_conv(h, c_w_in[j], c_conv_w[j], c_conv_b[j], c_ln_g[j], c_ln_b[j], c_w_out[j])
        h = layer_norm(ALPHA * h + y, post_ln_g[i], post_ln_b[i])
    return h
```

```python
import numpy as np
import ml_dtypes
import concourse.bass as bass
import concourse.mybir as mybir
from concourse.bass_utils import run_bass_kernel_spmd

F32 = mybir.dt.float32
BF16 = mybir.dt.bfloat16
AF = mybir.ActivationFunctionType
ALU = mybir.AluOpType

D = 2048
E = 4096
NCORE = 8
NTILE = 17
TMAX = 1152
ALPHA = float((2.0 * 4) ** 0.25)
EPS = 1e-5
PASSES = [(0, 9), (9, 17)]
UNIT = 4096
NUNIT = 6
NDS = 32
LXB = 45 * 1024

LAYER_GROUPS = [[0, 1, 2, 3]]


class Ev:
    __slots__ = ("sem", "val", "eng")

    def __init__(s, sem, val, eng):
        s.sem, s.val, s.eng = sem, val, eng


class Buf:
    __slots__ = ("name", "w", "r")

    def __init__(s, name=""):
        s.name = name
        s.w = None
        s.r = {}


class Eng:
    def __init__(s, nc, name, h):
        s.name = name
        s.h = h
        s.sem = nc.alloc_semaphore("pg_" + name)
        s.cnt = 0
        s.known = {}


def AP3(base, dims):
    return bass.AP(base.tensor, base.offset, [list(base.ap[0])] + [list(d) for d in dims])


class Prog:
    def __init__(s, layers, last_is_final=True):
        s.layers = layers
        nc = s.nc = bass.Bass("TRN2", target_bir_lowering=False)
        s.pe = Eng(nc, "pe", nc.tensor)
        s.act = Eng(nc, "act", nc.scalar)
        s.dve = Eng(nc, "dve", nc.vector)
        s.pool = Eng(nc, "pool", nc.gpsimd)
        s.sp = Eng(nc, "sp", nc.sync)
        s.dsems = [nc.alloc_semaphore(f"dq{i}") for i in range(NDS)]
        s.dval = [0] * NDS
        s.dlast = [None] * NDS
        s.di = 0
        s.nbank = 0
        s.out_events = []
        s.pending_loads = []

    def _wait(s, eng, ev):
        if ev is None:
            return
        k = ev.sem.num
        if eng.known.get(k, 0) >= ev.val:
            return
        if ev.eng is not None and ev.eng.cnt < ev.val:
            raise RuntimeError(f"pending event on {ev.eng.name}: {ev.val} > {ev.eng.cnt}")
        eng.h.wait_ge(ev.sem, ev.val)
        eng.known[k] = ev.val

    def _deps(s, eng, reads, writes):
        for b in reads:
            if b.w is not None:
                s._wait(eng, b.w)
        for b in writes:
            if b.w is not None and b.w.eng is not eng:
                s._wait(eng, b.w)
            for ev in b.r.values():
                if ev.eng is not eng:
                    s._wait(eng, ev)

    def _mark(s, ev, reads, writes):
        for b in writes:
            b.w = ev
            b.r = {}
        key = id(ev.eng) if ev.eng is not None else ("d", ev.sem.num)
        for b in reads:
            b.r[key] = ev

    def op(s, eng, fn, reads=(), writes=(), signal=True):
        s._deps(eng, reads, writes)
        ins = fn()
        if signal:
            ins.then_inc(eng.sem, 1)
            eng.cnt += 1
            ev = Ev(eng.sem, eng.cnt, eng)
        else:
            ev = Ev(eng.sem, eng.cnt + 1, eng)
        s._mark(ev, reads, writes)
        return ev

    def dma(s, q, out, in_, reads=(), writes=()):
        s._deps(q, reads, writes)
        i = s.di
        s.di = (s.di + 1) % NDS
        s._wait(q, s.dlast[i])
        q.h.dma_start(out=out, in_=in_).then_inc(s.dsems[i], 16)
        s.dval[i] += 16
        ev = Ev(s.dsems[i], s.dval[i], None)
        s.dlast[i] = ev
        s._mark(ev, reads, writes)
        return ev

    def A(s, fn, r=(), w=()):
        return s.op(s.act, fn, r, w)

    def V(s, fn, r=(), w=()):
        return s.op(s.dve, fn, r, w)

    def bank(s):
        i = s.nbank
        s.nbank = (s.nbank + 1) % 8
        return s.banks[i], s.bb[i]

    def build(s):
        nc = s.nc
        L = s.layers
        dt = lambda name, shape, dtype=F32, kind="ExternalInput": nc.dram_tensor(name, shape, dtype, kind=kind).ap()
        s.x_in = dt("x_in", [NTILE * 128, D])
        s.out = dt("out", [16 * 128, D], kind="ExternalOutput")
        s.cst_f = dt("cst_f", [128, 384])
        s.cst_bf = dt("cst_bf", [128, 256], BF16)
        s.post_rep = dt("post_rep", [4, 128, 2 * D])
        s.W = {}
        kinds = sorted(set(l % 3 for l in L))
        if 0 in kinds:
            s.W["a_w_in"] = dt("a_w_in", [2, D, 3 * E])
            s.W["a_w_out"] = dt("a_w_out", [2, E, D])
            s.W["a_wsT"] = dt("a_wsT", [2, 128, 1024])
            s.W["a_bs_rep"] = dt("a_bs_rep", [2, 128, 1024])
            s.W["a_vec"] = dt("a_vec", [2, 128, 64])
        if 1 in kinds:
            s.W["b_w_in"] = dt("b_w_in", [1, D, 2 * E])
            s.W["b_w_pool"] = dt("b_w_pool", [1, 4, 1024, 1024])
            s.W["b_w_out"] = dt("b_w_out", [1, E, D])
            s.W["b_vec"] = dt("b_vec", [1, 128, 32])
        if 2 in kinds:
            s.W["c_w_in"] = dt("c_w_in", [1, D, 3 * E])
            s.W["c_w_out"] = dt("c_w_out", [1, E, D])
            s.W["c_vec"] = dt("c_vec", [1, 128, 32 * 31 + 96])
        s.Xs = nc.dram_tensor("xs_scr", [NTILE * 128, D], F32, kind="Internal").ap()
        s.Hs = nc.dram_tensor("hs_scr", [NTILE * 128, D], F32, kind="Internal").ap()
        s.Xb = [Buf(f"X{j}") for j in range(NTILE)]
        s.Hb = [Buf(f"H{j}") for j in range(NTILE)]

        s.banks = [nc.alloc_psum_tensor(f"ps{i}", [128, 512], F32) for i in range(8)]
        s.bb = [Buf(f"bank{i}") for i in range(8)]
        s.hT = nc.alloc_sbuf_tensor("hT", [128, 16, TMAX], BF16)
        s.HTb = [Buf(f"hT{j}") for j in range(9)]
        s.G = nc.alloc_sbuf_tensor("G", [128, 9 * 4096], BF16)
        s.Gb = [[Buf(f"G{c}_{j}") for j in range(9)] for c in range(32)]
        s.GbT = [[s.Gb[c][j] for c in range(32)] for j in range(9)]
        s.WA = nc.alloc_sbuf_tensor("WA", [128, NUNIT * UNIT], BF16)
        s.Wb = [Buf(f"wa{i}") for i in range(NUNIT)]
        s.wptr = 0
        s.wown = [None] * NUNIT
        s.cur_tok = None
        s.LX = nc.alloc_sbuf_tensor("LX", [128, LXB // 2], BF16)
        s.lx_live = []
        s.vtail = nc.alloc_sbuf_tensor("vtail", [128, 32, 16], F32)
        s.vtb = [Buf() for _ in range(32)]
        s.gtail = nc.alloc_sbuf_tensor("gtail", [128, 32, 32], BF16)
        s.gtb = [Buf() for _ in range(32)]
        s.cf = nc.alloc_sbuf_tensor("cf", [128, 384], F32)
        s.cb = nc.alloc_sbuf_tensor("cb", [128, 256], BF16)
        s.cfb = Buf("cf")
        s.cbb = Buf("cb")
        s.sm = nc.alloc_sbuf_tensor("sm", [128, 2, 32], F32)
        s.smb = [Buf(), Buf()]

        s.dma(s.sp, s.cf[:, :], s.cst_f[:, :], (), (s.cfb,))
        s.dma(s.sp, s.cb[:, :], s.cst_bf[:, :], (), (s.cbb,))
        s.identb = s.cb[:, 0:128]
        s.onesb = s.cb[:, 128:256]
        s.tri = s.cf[:, 0:128]
        s.onesf = s.cf[:, 128:256]
        s.mask = s.cf[:, 256:257]
        s.invc = s.cf[:, 257:321]

        S = []
        for p, (g0, g1) in enumerate(PASSES):
            P = (p, g0, g1 - g0)
            for li, l in enumerate(L):
                is_last = li == len(L) - 1
                kind = l % 3
                out_needed = (p == 0) and any((m % 3) in (1, 2) for m in L[li + 1:])
                lo5 = 0 if (p != 0 or out_needed) else 1
                lo = 0 if (p != 0 or kind in (1, 2) or out_needed) else 1
                if li == 0:
                    S += s.phase0(P, lo)
                if kind == 0:
                    st, xp = s.layerA(P, l, lo)
                elif kind == 1:
                    st, xp = s.layerB(P, l, lo5)
                else:
                    st, xp = s.layerC(P, l, lo5)
                S += st
                S += s.phase45(P, l, lo5, is_last, xp, li == 0)
        s.run_stages(S)
        for ev in s.out_events:
            s._wait(s.sp, ev)
        return nc

    @staticmethod
    def tbs(lo, n):
        m = n - lo
        nb = (m + 3) // 4
        out = []
        j = lo
        for i in range(nb):
            k = m // nb + (1 if i < m % nb else 0)
            out.append((j, k))
            j += k
        return out

    def lx_begin(s):
        s.lx_off = 0
        s.lx_cur = []
        return s.lx_cur

    def lx_activate(s, bufs):
        fence = {}
        for b in s.lx_live:
            evs = list(b.r.values())
            if b.w is not None:
                evs.append(b.w)
            for ev in evs:
                key = id(ev.eng) if ev.eng is not None else ("d", ev.sem.num)
                if key not in fence or fence[key].val < ev.val:
                    fence[key] = ev
        for b in bufs:
            b.w = None
            b.r = dict(fence)
        s.lx_live = bufs

    def lx(s, nbytes, dtype=F32, name=""):
        nb = (nbytes + 31) // 32 * 32
        assert s.lx_off + nb <= LXB, (s.lx_off, nb, name)
        ap = s.LX[:, s.lx_off // 2:(s.lx_off + nbytes) // 2]
        s.lx_off += nb
        if dtype == F32:
            ap = ap.bitcast(F32)
        b = Buf(name)
        s.lx_cur.append(b)
        return ap, b

    def walloc(s, n):
        for attempt in range(3):
            if s.wptr + n > NUNIT:
                s.wptr = 0
            u = s.wptr
            bad = [i for i in range(u, u + n) if s.wown[i] is not None and not s.wown[i][0]]
            if not bad:
                break
            s.wptr = bad[-1] + 1
        else:
            raise RuntimeError("weight arena: no free units")
        for i in range(u, u + n):
            s.wown[i] = s.cur_tok
        s.wptr = u + n
        return u

    def wload(s, dram_ap, k, cols, nunits):
        assert k * cols == nunits * UNIT
        u = s.walloc(nunits)
        bufs = s.Wb[u:u + nunits]
        slot = s.WA[:, u * UNIT:(u + nunits) * UNIT].rearrange("p (k c) -> p k c", k=k)
        s.dma(s.pool, slot, dram_ap, (), bufs)
        return slot, bufs

    def run_stages(s, stages, look=2):
        ctxs = {}
        toks = {}
        n = len(stages)

        def ld(i):
            toks[i] = s.cur_tok = [False]
            ctxs[i] = stages[i][0]()
        for i in range(min(look, n)):
            ld(i)
        for i in range(n):
            stages[i][1](ctxs.pop(i))
            toks[i][0] = True
            if i + look < n:
                ld(i + look)

    def Gc(s, c, j0, k):
        o = j0 * 4096 + c * 128
        return AP3(s.G[:, o:o + 1], [[4096, k], [1, 128]])

    def yT(s, c, jj):
        o = jj * 4096 + c * 128
        return s.G[:, o:o + 128]

    def inproj(s, slot, sbufs, q, tb):
        j0, k = tb
        ntok = k * 128
        bk, bb = s.bank()
        hb = s.HTb[j0:j0 + k]
        for dc in range(16):
            s.op(s.pe, lambda dc=dc: s.nc.tensor.matmul(
                bk[:, 0:ntok], slot[:, dc, q * 128:(q + 1) * 128], s.hT[:, dc, j0 * 128:j0 * 128 + ntok],
                start=(dc == 0), stop=(dc == 15)),
                reads=list(sbufs) + hb, writes=[bb], signal=(dc == 15))
        return bk, bb

    def transposes(s, src, srcb, jj, dve_only=False):
        nc = s.nc
        for half in range(2):
            bk, bb = s.bank()
            bkb = bk[:, :].bitcast(BF16)
            for k in range(8):
                dc = half * 8 + k
                s.op(s.pe, lambda k=k, dc=dc: nc.tensor.transpose(
                    bkb[:, k * 128:(k + 1) * 128], src[:, dc * 128:(dc + 1) * 128], s.identb),
                    reads=list(srcb) + [s.cbb], writes=[bb], signal=(k == 7))
            dst = s.hT[:, half * 8:(half + 1) * 8, jj * 128:(jj + 1) * 128]
            srcv = bkb.rearrange("p (k t) -> p k t", k=8)
            if half == 0 and not dve_only:
                s.A(lambda: nc.scalar.copy(out=dst, in_=srcv), [bb], [s.HTb[jj]])
            else:
                s.V(lambda: nc.vector.tensor_copy(out=dst, in_=srcv), [bb], [s.HTb[jj]])

    def p5_bufs(s):
        f = lambda j: s.G[:, j * 4096:(j + 1) * 4096].bitcast(F32)
        return [(f(0), s.GbT[0], f(1), s.GbT[1]), (f(2), s.GbT[2], f(3), s.GbT[3]), (f(4), s.GbT[4], f(5), s.GbT[5])]

    def phase0(s, P, lo):
        p, g0, n = P
        stages = []
        for jj in range(lo, n):
            i = jj % 6
            hb = s.G[:, 6 * 4096 + i * 2048:6 * 4096 + (i + 1) * 2048]
            hbb = [s.Gb[c][6 + i // 2] for c in range((i % 2) * 16, (i % 2) * 16 + 16)]

            def loads(jj=jj, hb=hb, hbb=hbb):
                s.dma(s.pool, hb, s.x_in[(g0 + jj) * 128:(g0 + jj + 1) * 128, :], (), hbb)

            def compute(ctx, jj=jj, hb=hb, hbb=hbb):
                s.transposes(hb, hbb, jj)
            stages.append((loads, compute))
        return stages

    def phase45(s, P, l, lo5, is_last, xp, first_layer):
        nc = s.nc
        p, g0, n = P
        kind, j = l % 3, l // 3
        wout = s.W["abc"[kind] + "_w_out"][j].rearrange("(ec p) d -> p ec d", p=128)
        xpi = [0]
        xdone = {}
        src = s.x_in if first_layer else s.Hs
        pieces = [(b, jj) for b in range(8) for jj in range(lo5, n)]
        NXP = len(xp)

        def hload(i):
            if i >= len(pieces):
                return
            b, jj = pieces[i]
            gt = g0 + jj
            x_, xb_ = xp[i % NXP]
            s.dma(s.sp, x_, src[gt * 128:(gt + 1) * 128, b * 256:(b + 1) * 256], (s.Hb[gt],) if src is s.Hs else (), (xb_,))

        def p4_stage(b):
            def loads():
                return s.wload(wout[:, :, b * 256:(b + 1) * 256], 32, 256, 2)

            def compute(ctx):
                slot, sb = ctx
                for jj in range(lo5, n):
                    gt = g0 + jj
                    i = xpi[0]
                    assert pieces[i] == (b, jj)
                    if i == 0:
                        for k in range(NXP):
                            hload(k)
                        load_post()
                    bk, bb = s.bank()
                    for ec in range(32):
                        s.op(s.pe, lambda ec=ec: nc.tensor.matmul(
                            bk[:, 0:256], s.yT(ec, jj), slot[:, ec, :], start=(ec == 0), stop=(ec == 31)),
                            reads=list(sb) + [s.Gb[ec][jj]], writes=[bb], signal=(ec == 31))
                    x_, xb_ = xp[i % NXP]
                    xpi[0] += 1
                    s.V(lambda: nc.vector.scalar_tensor_tensor(out=x_, in0=x_, scalar=ALPHA, in1=bk[:, 0:256],
                                                               op0=ALU.mult, op1=ALU.add), [bb, xb_], [xb_])
                    s.dma(s.act, s.Xs[gt * 128:(gt + 1) * 128, b * 256:(b + 1) * 256], x_, (xb_,), (s.Xb[gt],))
                    xdone[gt] = xdone.get(gt, 0) + 1
                    hload(i + NXP)
                    if b == 7:
                        after_tile(jj)
            return (loads, compute)

        pb = s.p5_bufs()
        post = s.LX[:, 2048:2048 + 8192].bitcast(F32)
        pbuf = Buf("post")
        postb = [pbuf]
        hbs = [s.LX[:, 10240 + i * 2048:10240 + (i + 1) * 2048] for i in range(6)]
        hbb = [Buf(f"hb{i}") for i in range(6)]
        post_loaded = [False]
        pend = [None]

        def load_post():
            if not post_loaded[0]:
                fence = {}
                for b_ in s.lx_live:
                    for ev in list(b_.r.values()) + ([b_.w] if b_.w is not None else []):
                        key = id(ev.eng) if ev.eng is not None else ("d", ev.sem.num)
                        if key not in fence or fence[key].val < ev.val:
                            fence[key] = ev
                for nb_ in [pbuf] + hbb:
                    nb_.r = dict(fence)
                s.lx_live = list(s.lx_live) + [pbuf] + hbb
                s.dma(s.sp, post, s.post_rep[l], (), postb)
                post_loaded[0] = True

        def p5_loads(jj):
            gt = g0 + jj
            XA, XAb, HA, HAb = pb[jj % 3]
            load_post()
            assert xdone.get(gt, 0) == 8, (gt, xdone.get(gt, 0))
            s.dma(s.sp, XA, s.Xs[gt * 128:(gt + 1) * 128, :], (s.Xb[gt],), XAb)

        def p5_front(jj):
            gt = g0 + jj
            XA, XAb, HA, HAb = pb[jj % 3]
            sm = s.sm[:, jj % 2, :]
            smb = s.smb[jj % 2]
            st = sm[:, 0:24].rearrange("p (k s) -> p k s", k=4)
            for k in range(4):
                s.V(lambda k=k: nc.vector.bn_stats(out=st[:, k, :], in_=XA[:, k * 512:(k + 1) * 512]), XAb, [smb])
            mv = sm[:, 24:26]
            rs = sm[:, 26:27]
            nmr = sm[:, 27:28]
            s.V(lambda: nc.vector.bn_aggr(out=mv, in_=sm[:, 0:24]), [smb], [smb])
            s.V(lambda: nc.vector.tensor_scalar(out=rs, in0=mv[:, 1:2], scalar1=EPS, scalar2=None, op0=ALU.add), [smb], [smb])
            s.A(lambda: nc.scalar.activation(out=rs, in_=rs, func=AF.Sqrt), [smb], [smb])
            s.V(lambda: nc.vector.reciprocal(out=rs, in_=rs), [smb], [smb])
            s.V(lambda: nc.vector.tensor_scalar(out=nmr, in0=mv[:, 0:1], scalar1=rs, scalar2=-1.0, op0=ALU.mult, op1=ALU.mult),
                [smb], [smb])
            s.A(lambda: nc.scalar.activation(out=HA, in_=XA, func=AF.Identity, bias=nmr, scale=rs), XAb + [smb], HAb)
            s.V(lambda: nc.vector.tensor_tensor(out=HA, in0=HA, in1=post[:, 0:D], op=ALU.mult), HAb + postb, HAb)
            s.op(s.pool, lambda: nc.gpsimd.tensor_tensor(out=XA, in0=HA, in1=post[:, D:2 * D], op=ALU.add), HAb + postb, XAb)
            if is_last:
                ev = s.dma(s.pool, s.out[(gt - 1) * 128:gt * 128, :], XA, XAb, ())
                s.out_events.append(ev)
            else:
                s.dma(s.pool, s.Hs[gt * 128:(gt + 1) * 128, :], XA, XAb, (s.Hb[gt],))

        def run_pend():
            if pend[0] is not None:
                f = pend[0]
                pend[0] = None
                f()

        started = []
        loaded = []
        cast_done = set()
        front_done = set()
        next_t = [lo5]
        pcast = [None]

        def run_pcast():
            if pcast[0] is not None:
                f = pcast[0]
                pcast[0] = None
                f()

        def mk_cast(jj, slot):
            def f():
                XA, XAb, HA, HAb = pb[jj % 3]
                s.A(lambda: nc.scalar.copy(out=hbs[slot], in_=XA), XAb, [hbb[slot]])
                cast_done.add(jj)
            return f

        tiles = list(range(lo5, n))
        lset = set()

        def can_load(u):
            if u in lset or u >= n:
                return False
            if u - 3 < lo5:
                return True
            return (u - 3) in (front_done if is_last else cast_done)

        def after_tile(jj):
            while next_t[0] < n and len(lset) < 3:
                t = next_t[0]
                if jj < max(t, 2 * (t % 3) + 1) or not can_load(t):
                    break
                p5_loads(t)
                lset.add(t)
                next_t[0] += 1

        def flushT(ctx):
            prev = None
            for u in tiles[:2]:
                if can_load(u):
                    p5_loads(u)
                    lset.add(u)
            for t in tiles:
                if t not in lset:
                    assert can_load(t), t
                    p5_loads(t)
                    lset.add(t)
                p5_front(t)
                front_done.add(t)
                if not is_last:
                    if prev is not None:
                        mk_cast(prev, (prev - lo5) % 6)()
                        s.transposes(hbs[(prev - lo5) % 6], [hbb[(prev - lo5) % 6]], prev, dve_only=True)
                    prev = t
                for u in (t + 1, t + 2):
                    if can_load(u):
                        p5_loads(u)
                        lset.add(u)
            if not is_last and prev is not None:
                mk_cast(prev, (prev - lo5) % 6)()
                s.transposes(hbs[(prev - lo5) % 6], [hbb[(prev - lo5) % 6]], prev, dve_only=True)

        stages = [p4_stage(b) for b in range(8)]
        stages.append((lambda: None, flushT))
        return stages

    def layerA(s, P, l, lo):
        nc = s.nc
        p, g0, n = P
        j = l // 3
        win = s.W["a_w_in"][j].rearrange("(dc p) c -> p dc c", p=128)
        bufs = s.lx_begin()
        xp = [s.lx(1024, F32, f"xp{i}") for i in range(4)]
        wsTb, wsTbb = s.lx(2048, BF16, "wsT")
        wsum, wsumb = s.lx(4096, F32, "wsum")
        bsr, bsrb = s.lx(4096, F32, "bsr")
        vec, vecb = s.lx(256, F32, "vecA")
        stats, statsb = s.lx(9 * 48 * 4, F32, "stats")
        mvs, mvsb = s.lx(9 * 4 * 4, F32, "mvs")
        Bc = [s.lx(512, F32, f"Bc{i}") for i in range(4)]
        scr = [s.lx(2048, F32, f"scr{i}") for i in range(6)]
        wtmp, wtmpb = s.lx(4096, F32, "wtmp")
        lng = vec[:, 0:32]
        lnb = vec[:, 32:64]

        def setup(ctx):
            s.lx_activate(bufs)
            s.dma(s.sp, wtmp, s.W["a_wsT"][j], (), (wtmpb,))
            s.dma(s.sp, bsr, s.W["a_bs_rep"][j], (), (bsrb,))
            s.dma(s.sp, vec, s.W["a_vec"][j], (), (vecb,))
            w3 = wtmp.rearrange("p (h t) -> p h t", h=8)
            s.V(lambda: nc.vector.tensor_tensor(out=w3, in0=w3, in1=AP3(s.tri, [[0, 8], [1, 128]]), op=ALU.mult),
                [wtmpb, s.cfb], [wtmpb])
            s.V(lambda: nc.vector.tensor_copy(out=wsTb, in_=wtmp), [wtmpb], [wsTbb])
            for hh in range(2):
                bk, bb = s.bank()
                s.op(s.pe, lambda hh=hh: nc.tensor.matmul(bk[:, :], s.onesf, wtmp[:, hh * 512:(hh + 1) * 512], start=True, stop=True),
                     reads=[wtmpb, s.cfb], writes=[bb])
                s.V(lambda hh=hh: nc.vector.tensor_copy(out=wsum[:, hh * 512:(hh + 1) * 512], in_=bk[:, :]), [bb], [wsumb])

        stages = [(lambda: None, setup)]

        def Gtm(hd, jj):
            o = jj * 4096 + hd * 512
            return s.G[:, o:o + 512]

        for hd in range(8):
            def loads(hd=hd):
                return s.wload(win[:, :, E + hd * 512:E + (hd + 1) * 512], 16, 512, 2)

            def compute(ctx, hd=hd):
                slot, sb = ctx
                for jj in range(lo, n):
                    bk, bb = s.bank()
                    for dc in range(16):
                        s.op(s.pe, lambda dc=dc: nc.tensor.matmul(
                            bk[:, :], s.hT[:, dc, jj * 128:(jj + 1) * 128], slot[:, dc, :], start=(dc == 0), stop=(dc == 15)),
                            reads=list(sb) + [s.HTb[jj]], writes=[bb], signal=(dc == 15))
                    gb = [s.Gb[hd * 4 + q][jj] for q in range(4)]
                    s.A(lambda: nc.scalar.activation(out=Gtm(hd, jj), in_=bk[:, :], func=AF.Gelu_apprx_tanh), [bb], gb)
                    so = (jj * 8 + hd) * 6
                    s.V(lambda: nc.vector.bn_stats(out=stats[:, so:so + 6], in_=Gtm(hd, jj)), gb, [statsb])
            stages.append((loads, compute))

        def p2(ctx):
            for jj in range(lo, n):
                mv = mvs[:, jj * 4:jj * 4 + 2]
                rs = mvs[:, jj * 4 + 2:jj * 4 + 3]
                s.V(lambda: nc.vector.bn_aggr(out=mv, in_=stats[:, jj * 48:(jj + 1) * 48]), [statsb], [mvsb])
                s.V(lambda: nc.vector.tensor_scalar(out=rs, in0=mv[:, 1:2], scalar1=EPS, scalar2=None, op0=ALU.add), [mvsb], [mvsb])
                s.A(lambda: nc.scalar.activation(out=rs, in_=rs, func=AF.Sqrt), [mvsb], [mvsb])
                s.V(lambda: nc.vector.reciprocal(out=rs, in_=rs), [mvsb], [mvsb])
                gv = s.G[:, jj * 4096:(jj + 1) * 4096]
                gb = s.GbT[jj]
                s.V(lambda: nc.vector.tensor_scalar(out=gv, in0=gv, scalar1=mv[:, 0:1], scalar2=rs,
                                                    op0=ALU.subtract, op1=ALU.mult), gb + [mvsb], gb)
        stages.append((lambda: None, p2))

        unit = [0]
        for cp in range(16):
            def loads(cp=cp):
                c0 = cp * 2
                us = s.wload(win[:, :, c0 * 128:c0 * 128 + 256], 16, 256, 1)
                zs = s.wload(win[:, :, 2 * E + c0 * 128:2 * E + c0 * 128 + 256], 16, 256, 1)
                return us, zs

            def compute(ctx, cp=cp):
                (us, usb), (zs, zsb) = ctx
                for q in range(2):
                    c = cp * 2 + q
                    hd, qq = c // 4, c % 4
                    bc, bcb = Bc[c % 4]
                    s.V(lambda: nc.vector.scalar_tensor_tensor(
                        out=bc, in0=wsum[:, hd * 128:(hd + 1) * 128], scalar=lnb[:, c:c + 1],
                        in1=bsr[:, hd * 128:(hd + 1) * 128], op0=ALU.mult, op1=ALU.add),
                        [wsumb, bsrb, vecb], [bcb])
                    for tb in s.tbs(lo, n):
                        j0, k = tb
                        ntok = k * 128
                        bu, bub = s.inproj(us, usb, q, tb)
                        bz, bzb = s.inproj(zs, zsb, q, tb)
                        bm, bmb = s.bank()
                        for i in range(k):
                            jj = j0 + i
                            s.op(s.pe, lambda i=i, jj=jj: nc.tensor.matmul(
                                bm[:, i * 128:(i + 1) * 128], s.yT(c, jj),
                                wsTb[:, hd * 128:(hd + 1) * 128], start=True, stop=True),
                                reads=[s.Gb[c][jj], wsTbb], writes=[bmb], signal=(i == k - 1))
                        u0 = (unit[0] % 2) * 3
                        unit[0] += 1
                        (s1, s1b), (s2, s2b), (s3, s3b) = scr[u0], scr[u0 + 1], scr[u0 + 2]
                        r3 = lambda a: a[:, 0:ntok].rearrange("p (a t) -> p a t", a=k)
                        s.A(lambda: nc.scalar.activation(out=s1[:, 0:ntok], in_=bu[:, 0:ntok], func=AF.Gelu_apprx_tanh), [bub], [s1b])
                        s.A(lambda: nc.scalar.activation(out=s2[:, 0:ntok], in_=bz[:, 0:ntok], func=AF.Silu), [bzb], [s2b])
                        s.V(lambda: nc.vector.scalar_tensor_tensor(
                            out=r3(s3), in0=r3(bm), scalar=lng[:, c:c + 1],
                            in1=AP3(bc, [[0, k], [1, 128]]), op0=ALU.mult, op1=ALU.add),
                            [bmb, bcb, vecb], [s3b])
                        s.V(lambda: nc.vector.tensor_tensor(out=s1[:, 0:ntok], in0=s1[:, 0:ntok], in1=s3[:, 0:ntok], op=ALU.mult),
                            [s1b, s3b], [s1b])
                        gb = [s.Gb[c][j0 + i] for i in range(k)]
                        s.V(lambda: nc.vector.tensor_tensor(out=s.Gc(c, j0, k), in0=r3(s1), in1=r3(s2), op=ALU.mult),
                            [s1b, s2b], gb)
            stages.append((loads, compute))
        return stages, [(a[:, 0:256], b) for a, b in xp]

    def layerB(s, P, l, lo5):
        nc = s.nc
        p, g0, n = P
        T = n * 128
        p0 = p == 0
        win = s.W["b_w_in"][0].rearrange("(dc p) c -> p dc c", p=128)
        bufs = s.lx_begin()
        xp = [s.lx(1024, F32, f"xp{i}") for i in range(4)]
        pT, _ = s.lx(8 * TMAX * 2, BF16, "pT")
        pT3 = pT.rearrange("p (c t) -> p c t", c=8)
        pTbs = [Buf() for _ in range(8)]
        bufs += pTbs
        PADL = 16 + TMAX
        vp = [s.lx(PADL * 4, F32, f"vp{i}") for i in range(2)]
        wa, wab = s.lx(PADL * 4, F32, "wa")
        wb_, wbb = s.lx(PADL * 4, F32, "wb")
        vec, vecb = s.lx(128, F32, "vecB")
        fix, fixb = s.lx(64, F32, "fix")
        scr = [s.lx(2048, F32, f"scr{i}") for i in range(2)]
        L = 16 + T

        def setup(ctx):
            s.lx_activate(bufs)
            s.dma(s.sp, vec, s.W["b_vec"][0], (), (vecb,))
        stages = [(lambda: None, setup)]
        cnt = [0]
        for g in range(4):
            w = 2 ** (g + 1)
            for sp_ in range(4):
                def loads(g=g, sp_=sp_):
                    c0 = g * 8 + sp_ * 2
                    return s.wload(win[:, :, c0 * 128:c0 * 128 + 256], 16, 256, 1)

                def compute(ctx, g=g, sp_=sp_, w=w):
                    slot, sb = ctx
                    for q in range(2):
                        c = g * 8 + sp_ * 2 + q
                        cc = c - g * 8
                        v, vb = vp[cnt[0] % 2]
                        cnt[0] += 1
                        if p0:
                            s.V(lambda: nc.vector.memset(v[:, 0:16], 0.0), [], [vb])
                        else:
                            s.V(lambda: nc.vector.tensor_copy(out=v[:, 0:16], in_=s.vtail[:, c, :]), [s.vtb[c]], [vb])
                        for tb in s.tbs(0, n):
                            j0, k = tb
                            bk, bb = s.inproj(slot, sb, q, tb)
                            s.A(lambda: nc.scalar.copy(out=v[:, 16 + j0 * 128:16 + (j0 + k) * 128], in_=bk[:, 0:k * 128]), [bb], [vb])
                        if p0:
                            s.V(lambda: nc.vector.tensor_scalar(out=v[:, 16:144], in0=v[:, 16:144], scalar1=s.mask, scalar2=None,
                                                                op0=ALU.mult), [vb, s.cfb], [vb])
                            s.V(lambda: nc.vector.tensor_copy(out=s.vtail[:, c, :], in_=v[:, L - 16:L]), [vb], [s.vtb[c]])
                        cur, curb = v, vb
                        m = 1
                        while m < w:
                            o, ob = (wa, wab) if cur is not wa else (wb_, wbb)
                            s.V(lambda cur=cur, o=o, m=m: nc.vector.tensor_tensor(
                                out=o[:, 2 * m - 1:L], in0=cur[:, 2 * m - 1:L], in1=cur[:, m - 1:L - m], op=ALU.add),
                                [curb], [ob])
                            cur, curb = o, ob
                            m *= 2
                        s.V(lambda cur=cur: nc.vector.scalar_tensor_tensor(
                            out=pT3[:, cc, 0:T], in0=cur[:, 16:L], scalar=1.0 / w, in1=v[:, 16:L],
                            op0=ALU.mult, op1=ALU.subtract), [curb, vb], [pTbs[cc]])
                        if p0:
                            s.V(lambda cur=cur: nc.vector.tensor_tensor(out=fix[:, 0:16], in0=cur[:, 144:160],
                                                                        in1=s.invc[:, g * 16:(g + 1) * 16], op=ALU.mult),
                                [curb, s.cfb], [fixb])
                            s.V(lambda: nc.vector.tensor_tensor(out=pT3[:, cc, 128:144], in0=fix[:, 0:16], in1=v[:, 144:160],
                                                                op=ALU.subtract), [fixb, vb], [pTbs[cc]])
                stages.append((loads, compute))
            for dq in range(2):
                def loads(g=g, dq=dq):
                    e0 = g * 1024 + dq * 512
                    wp = s.W["b_w_pool"][0, g].rearrange("(cc p) d -> p cc d", p=128)
                    a = s.wload(wp[:, :, dq * 512:(dq + 1) * 512], 8, 512, 1)
                    z0 = s.wload(win[:, :, E + e0:E + e0 + 256], 16, 256, 1)
                    z1 = s.wload(win[:, :, E + e0 + 256:E + e0 + 512], 16, 256, 1)
                    return a, z0, z1

                def compute(ctx, g=g, dq=dq):
                    (wp, wpb), z0, z1 = ctx
                    for dch in range(4):
                        c = g * 8 + dq * 4 + dch
                        zs, zsb = z0 if dch < 2 else z1
                        for tb in s.tbs(lo5, n):
                            j0, k = tb
                            ntok = k * 128
                            bq, bqb = s.bank()
                            for cc in range(8):
                                s.op(s.pe, lambda cc=cc: nc.tensor.matmul(
                                    bq[:, 0:ntok], wp[:, cc, dch * 128:(dch + 1) * 128], pT3[:, cc, j0 * 128:j0 * 128 + ntok],
                                    start=(cc == 0), stop=(cc == 7)),
                                    reads=list(wpb) + [pTbs[cc]], writes=[bqb], signal=(cc == 7))
                            bz, bzb = s.inproj(zs, zsb, dch % 2, tb)
                            sz, szb = scr[cnt[0] % 2]
                            cnt[0] += 1
                            r3 = lambda a: a[:, 0:ntok].rearrange("p (a t) -> p a t", a=k)
                            s.A(lambda: nc.scalar.activation(out=sz[:, 0:ntok], in_=bz[:, 0:ntok], func=AF.Silu), [bzb], [szb])
                            gb = [s.Gb[c][j0 + i] for i in range(k)]
                            s.V(lambda: nc.vector.scalar_tensor_tensor(
                                out=s.Gc(c, j0, k), in0=r3(bq), scalar=vec[:, c:c + 1], in1=r3(sz),
                                op0=ALU.mult, op1=ALU.mult), [bqb, szb, vecb], gb)
                stages.append((loads, compute))
        return stages, [(a[:, 0:256], b) for a, b in xp]

    def layerC(s, P, l, lo5):
        nc = s.nc
        p, g0, n = P
        T = n * 128
        p0 = p == 0
        win = s.W["c_w_in"][0].rearrange("(dc p) c -> p dc c", p=128)
        bufs = s.lx_begin()
        xp = [s.lx(1024, F32, f"xp{i}") for i in range(3)]
        PADL = 32 + TMAX
        gp = [s.lx(PADL * 2, BF16, f"gp{i}") for i in range(2)]
        dg = [s.lx(31 * 128 * 2, BF16, f"dg{i}") for i in range(2)]
        MU, MUb = s.lx(TMAX * 4, F32, "MU")
        RS, RSb = s.lx(TMAX * 4, F32, "RS")
        vec, vecb = s.lx((32 * 31 + 96) * 4, F32, "vecC")
        sq = [s.lx(1024, BF16, f"sq{i}") for i in range(2)]
        scr = [s.lx(2048, F32, f"scr{i}") for i in range(3)]
        cw = vec[:, 0:32 * 31]
        convb = vec[:, 32 * 31:32 * 31 + 32]
        lng = vec[:, 32 * 31 + 32:32 * 31 + 64]
        lnb = vec[:, 32 * 31 + 64:32 * 31 + 96]
        L = 32 + T

        def setup(ctx):
            s.lx_activate(bufs)
            s.dma(s.sp, vec, s.W["c_vec"][0], (), (vecb,))
        stages = [(lambda: None, setup)]
        cnt = [0]
        prev = [None]

        NDV = 6
        NPE = 31 - NDV

        def conv(c):
            g_, gb_ = gp[c % 2]
            d3, db = dg[c % 2]
            d3 = d3.rearrange("p (k e) -> p k e", k=31)
            for tb in s.tbs(0, n):
                j0, k = tb
                ntok = k * 128
                bc, bcb = s.bank()
                for kk in range(NPE):
                    st = 32 + j0 * 128 - 30 + kk
                    s.op(s.pe, lambda kk=kk, st=st: nc.tensor.matmul(
                        bc[:, 0:ntok], d3[:, kk, :], g_[:, st:st + ntok], start=(kk == 0), stop=(kk == NPE - 1)),
                        reads=[db, gb_], writes=[bcb], signal=(kk == NPE - 1))
                sc, scb = scr[cnt[0] % 3]
                cnt[0] += 1
                for i, kk in enumerate(range(NPE, 31)):
                    st = 32 + j0 * 128 - 30 + kk
                    src_ = bc[:, 0:ntok] if i == 0 else sc[:, 0:ntok]
                    s.V(lambda kk=kk, st=st, src_=src_: nc.vector.scalar_tensor_tensor(
                        out=sc[:, 0:ntok], in0=g_[:, st:st + ntok], scalar=cw[:, c * 31 + kk:c * 31 + kk + 1], in1=src_,
                        op0=ALU.mult, op1=ALU.add), [gb_, vecb] + ([bcb] if i == 0 else [scb]), [scb])
                gb = [s.Gb[c][j0 + i] for i in range(k)]
                s.A(lambda: nc.scalar.activation(out=s.Gc(c, j0, k), in_=sc[:, 0:ntok].rearrange("p (a t) -> p a t", a=k),
                                                 func=AF.Identity, bias=convb[:, c:c + 1], scale=1.0), [scb, vecb], gb)

        for cp in range(16):
            def loads(cp=cp):
                c0 = cp * 2
                a = s.wload(win[:, :, c0 * 128:c0 * 128 + 256], 16, 256, 1)
                g = s.wload(win[:, :, E + c0 * 128:E + c0 * 128 + 256], 16, 256, 1)
                return a, g

            def compute(ctx, cp=cp):
                (as_, asb), (gs, gsb) = ctx
                for q in range(2):
                    c = cp * 2 + q
                    g_, gb_ = gp[c % 2]
                    d_, db = dg[c % 2]
                    if p0:
                        s.V(lambda: nc.vector.memset(g_[:, 0:32], 0.0), [], [gb_])
                    else:
                        s.V(lambda: nc.vector.tensor_copy(out=g_[:, 0:32], in_=s.gtail[:, c, :]), [s.gtb[c]], [gb_])
                    s.V(lambda: nc.vector.tensor_tensor(
                        out=d_.rearrange("p (k e) -> p k e", k=31)[:, 0:NPE, :], in0=AP3(s.identb, [[0, NPE], [1, 128]]),
                        in1=AP3(cw[:, c * 31:c * 31 + 1], [[1, NPE], [0, 128]]), op=ALU.mult),
                        [s.cbb, vecb], [db])
                    for tb in s.tbs(0, n):
                        j0, k = tb
                        ntok = k * 128
                        ba, bab = s.inproj(as_, asb, q, tb)
                        bg, bgb = s.inproj(gs, gsb, q, tb)
                        sg, sgb = scr[cnt[0] % 3]
                        cnt[0] += 1
                        s.A(lambda: nc.scalar.activation(out=sg[:, 0:ntok], in_=bg[:, 0:ntok], func=AF.Sigmoid), [bgb], [sgb])
                        s.V(lambda: nc.vector.tensor_tensor(out=g_[:, 32 + j0 * 128:32 + j0 * 128 + ntok], in0=ba[:, 0:ntok],
                                                            in1=sg[:, 0:ntok], op=ALU.mult), [bab, sgb], [gb_])
                    if p0:
                        s.V(lambda: nc.vector.tensor_scalar(out=g_[:, 32:160], in0=g_[:, 32:160], scalar1=s.mask, scalar2=None,
                                                            op0=ALU.mult), [gb_, s.cfb], [gb_])
                        s.V(lambda: nc.vector.tensor_copy(out=s.gtail[:, c, :], in_=g_[:, L - 32:L]), [gb_], [s.gtb[c]])
                    if prev[0] is not None:
                        conv(prev[0])
                    prev[0] = c
            stages.append((loads, compute))

        def p2(ctx):
            conv(prev[0])
            for tb in s.tbs(lo5, n):
                j0, k = tb
                ntok = k * 128
                b1, b1b = s.bank()
                b2, b2b = s.bank()
                for c in range(32):
                    q_, qb = sq[c % 2]
                    gb = [s.Gb[c][j0 + i] for i in range(k)]
                    s.A(lambda: nc.scalar.activation(out=q_[:, 0:ntok].rearrange("p (a t) -> p a t", a=k), in_=s.Gc(c, j0, k),
                                                     func=AF.Square), gb, [qb])
                    s.op(s.pe, lambda: nc.tensor.matmul(b1[:, 0:ntok], s.onesb, s.Gc(c, j0, k), start=(c == 0), stop=(c == 31)),
                         reads=gb + [s.cbb], writes=[b1b], signal=(c == 31))
                    s.op(s.pe, lambda: nc.tensor.matmul(b2[:, 0:ntok], s.onesb, q_[:, 0:ntok], start=(c == 0), stop=(c == 31)),
                         reads=[qb, s.cbb], writes=[b2b], signal=True)
                mu = MU[:, j0 * 128:j0 * 128 + ntok]
                rs = RS[:, j0 * 128:j0 * 128 + ntok]
                t_, tb_ = scr[0]
                s.A(lambda: nc.scalar.activation(out=mu, in_=b1[:, 0:ntok], func=AF.Identity, scale=1.0 / E), [b1b], [MUb])
                s.V(lambda: nc.vector.tensor_tensor(out=t_[:, 0:ntok], in0=mu, in1=mu, op=ALU.mult), [MUb], [tb_])
                s.V(lambda: nc.vector.scalar_tensor_tensor(out=rs, in0=b2[:, 0:ntok], scalar=1.0 / E, in1=t_[:, 0:ntok],
                                                           op0=ALU.mult, op1=ALU.subtract), [b2b, tb_], [RSb])
                s.V(lambda: nc.vector.tensor_scalar(out=rs, in0=rs, scalar1=EPS, scalar2=None, op0=ALU.add), [RSb], [RSb])
                s.A(lambda: nc.scalar.activation(out=rs, in_=rs, func=AF.Sqrt), [RSb], [RSb])
                s.V(lambda: nc.vector.reciprocal(out=rs, in_=rs), [RSb], [RSb])
        stages.append((lambda: None, p2))

        for cp in range(16):
            def loads(cp=cp):
                c0 = cp * 2
                return s.wload(win[:, :, 2 * E + c0 * 128:2 * E + c0 * 128 + 256], 16, 256, 1)

            def compute(ctx, cp=cp):
                zs, zsb = ctx
                for q in range(2):
                    c = cp * 2 + q
                    for tb in s.tbs(lo5, n):
                        j0, k = tb
                        ntok = k * 128
                        bz, bzb = s.inproj(zs, zsb, q, tb)
                        i0 = cnt[0] % 3
                        cnt[0] += 1
                        (s1, s1b) = scr[i0]
                        (s2, s2b) = scr[(i0 + 1) % 3]
                        r3 = lambda a: a[:, 0:ntok].rearrange("p (a t) -> p a t", a=k)
                        gb = [s.Gb[c][j0 + i] for i in range(k)]
                        s.A(lambda: nc.scalar.activation(out=s1[:, 0:ntok], in_=bz[:, 0:ntok], func=AF.Silu), [bzb], [s1b])
                        s.V(lambda: nc.vector.tensor_tensor(out=r3(s2), in0=s.Gc(c, j0, k),
                                                            in1=r3(MU[:, j0 * 128:j0 * 128 + ntok]), op=ALU.subtract), gb + [MUb], [s2b])
                        s.V(lambda: nc.vector.tensor_tensor(out=s2[:, 0:ntok], in0=s2[:, 0:ntok],
                                                            in1=RS[:, j0 * 128:j0 * 128 + ntok], op=ALU.mult), [s2b, RSb], [s2b])
                        s.A(lambda: nc.scalar.activation(out=s2[:, 0:ntok], in_=s2[:, 0:ntok], func=AF.Silu,
                                                         bias=lnb[:, c:c + 1], scale=lng[:, c:c + 1]), [s2b, vecb], [s2b])
                        s.V(lambda: nc.vector.tensor_tensor(out=s.Gc(c, j0, k), in0=r3(s2), in1=r3(s1), op=ALU.mult),
                            [s1b, s2b], gb)
            stages.append((loads, compute))
        return stages, [(a[:, 0:256], b) for a, b in xp]


_PROGS = {}


def _get_prog(layers):
    key = tuple(layers)
    if key not in _PROGS:
        p = Prog(list(layers))
        p.build()
        _PROGS[key] = p
    return _PROGS[key]


def _consts():
    cb = np.zeros((128, 256), dtype=ml_dtypes.bfloat16)
    cb[:, 0:128] = np.eye(128, dtype=np.float32).astype(ml_dtypes.bfloat16)
    cb[:, 128:256] = np.ones((128, 128), dtype=np.float32).astype(ml_dtypes.bfloat16)
    cfs = []
    for core in range(NCORE):
        cf = np.zeros((128, 384), dtype=np.float32)
        s_idx = np.arange(128)[:, None]
        t_idx = np.arange(128)[None, :]
        cf[:, 0:128] = (s_idx <= t_idx).astype(np.float32)
        cf[:, 128:256] = 1.0
        start = (core % 2 == 0)
        cf[:, 256] = 0.0 if start else 1.0
        for g, w in enumerate((2, 4, 8, 16)):
            pos = np.arange(16)
            cnt = np.minimum(pos + 1, w) if start else np.full(16, w)
            cf[:, 257 + g * 16:257 + (g + 1) * 16] = (1.0 / cnt.astype(np.float32))[None, :]
        cfs.append(cf)
    return cb, cfs


def _fm(v):
    return np.ascontiguousarray(v.reshape(32, 128).T)


def kernel(x, a_w_in, a_ln_g, a_ln_b, a_w_s, a_b_s, a_w_out, b_w_in, b_w_pool, b_scale, b_w_out,
           c_w_in, c_conv_w, c_conv_b, c_ln_g, c_ln_b, c_w_out, post_ln_g, post_ln_b):
    f = lambda a: np.ascontiguousarray(np.asarray(a, dtype=np.float32))
    x = f(x)
    cb, cfs = _consts()
    post_rep = np.empty((4, 128, 2 * D), dtype=np.float32)
    post_rep[:, :, 0:D] = f(post_ln_g)[:, None, :]
    post_rep[:, :, D:] = f(post_ln_b)[:, None, :]
    shared = {"cst_bf": cb, "post_rep": post_rep}
    shared["a_w_in"] = f(a_w_in)
    shared["a_w_out"] = f(a_w_out)
    shared["a_wsT"] = np.ascontiguousarray(f(a_w_s).transpose(0, 3, 1, 2).reshape(2, 128, 1024))
    bs = f(a_b_s).reshape(2, 1, 1024)
    shared["a_bs_rep"] = np.ascontiguousarray(np.broadcast_to(bs, (2, 128, 1024)))
    av = np.empty((2, 128, 64), dtype=np.float32)
    for j in range(2):
        av[j, :, 0:32] = _fm(f(a_ln_g)[j])
        av[j, :, 32:64] = _fm(f(a_ln_b)[j])
    shared["a_vec"] = av
    shared["b_w_in"] = f(b_w_in)
    shared["b_w_pool"] = f(b_w_pool)
    shared["b_w_out"] = f(b_w_out)
    shared["b_vec"] = np.ascontiguousarray(_fm(f(b_scale)[0])[None])
    shared["c_w_in"] = f(c_w_in)
    shared["c_w_out"] = f(c_w_out)
    cv = np.empty((1, 128, 32 * 31 + 96), dtype=np.float32)
    cw = f(c_conv_w)[0]
    cv[0, :, 0:32 * 31] = cw.reshape(31, 32, 128).transpose(2, 1, 0).reshape(128, 32 * 31)
    cv[0, :, 32 * 31:32 * 31 + 32] = _fm(f(c_conv_b)[0])
    cv[0, :, 32 * 31 + 32:32 * 31 + 64] = _fm(f(c_ln_g)[0])
    cv[0, :, 32 * 31 + 64:32 * 31 + 96] = _fm(f(c_ln_b)[0])
    shared["c_vec"] = cv

    h = x.reshape(4 * 4096, D)
    for layers in LAYER_GROUPS:
        prog = _get_prog(layers)
        in_maps = []
        for core in range(NCORE):
            r0 = core * 2048
            xin = np.zeros((NTILE * 128, D), dtype=np.float32)
            xin[128:] = h[r0:r0 + 2048]
            if core % 2 == 1:
                xin[0:128] = h[r0 - 128:r0]
            m = {"x_in": xin, "cst_f": cfs[core]}
            for k in ["cst_bf", "post_rep"] + list(prog.W.keys()):
                m[k] = shared[k]
            in_maps.append(m)
        res = run_bass_kernel_spmd(prog.nc, in_maps, core_ids=list(range(NCORE)))
        h = np.concatenate([np.asarray(r["out"]) for r in res.results], axis=0)
    return h.reshape(4, 4096, D).astype(np.float32)
```

```python
import numpy as np
import ml_dtypes
import concourse.bass as bass
import concourse.mybir as mybir
from concourse.bass_utils import run_bass_kernel_spmd

F32 = mybir.dt.float32
BF16 = mybir.dt.bfloat16
AF = mybir.ActivationFunctionType
ALU = mybir.AluOpType

D = 2048
E = 4096
NCORE = 8
NTILE = 17
TMAX = 1152
ALPHA = float((2.0 * 4) ** 0.25)
EPS = 1e-5
PASSES = [(0, 9), (9, 17)]
UNIT = 4096
NUNIT = 6
NDS = 32
LXB = 45 * 1024

LAYER_GROUPS = [[0, 1, 2, 3]]


class Ev:
    __slots__ = ("sem", "val", "eng")

    def __init__(s, sem, val, eng):
        s.sem, s.val, s.eng = sem, val, eng


class Buf:
    __slots__ = ("name", "w", "r")

    def __init__(s, name=""):
        s.name = name
        s.w = None
        s.r = {}


class Eng:
    def __init__(s, nc, name, h):
        s.name = name
        s.h = h
        s.sem = nc.alloc_semaphore("pg_" + name)
        s.cnt = 0
        s.known = {}


def AP3(base, dims):
    return bass.AP(base.tensor, base.offset, [list(base.ap[0])] + [list(d) for d in dims])


class Prog:
    def __init__(s, layers, last_is_final=True):
        s.layers = layers
        nc = s.nc = bass.Bass("TRN2", target_bir_lowering=False)
        s.pe = Eng(nc, "pe", nc.tensor)
        s.act = Eng(nc, "act", nc.scalar)
        s.dve = Eng(nc, "dve", nc.vector)
        s.pool = Eng(nc, "pool", nc.gpsimd)
        s.sp = Eng(nc, "sp", nc.sync)
        s.dsems = [nc.alloc_semaphore(f"dq{i}") for i in range(NDS)]
        s.dval = [0] * NDS
        s.dlast = [None] * NDS
        s.di = 0
        s.nbank = 0
        s.out_events = []
        s.pending_loads = []

    def _wait(s, eng, ev):
        if ev is None:
            return
        k = ev.sem.num
        if eng.known.get(k, 0) >= ev.val:
            return
        if ev.eng is not None and ev.eng.cnt < ev.val:
            raise RuntimeError(f"pending event on {ev.eng.name}: {ev.val} > {ev.eng.cnt}")
        eng.h.wait_ge(ev.sem, ev.val)
        eng.known[k] = ev.val

    def _deps(s, eng, reads, writes):
        for b in reads:
            if b.w is not None:
                s._wait(eng, b.w)
        for b in writes:
            if b.w is not None and b.w.eng is not eng:
                s._wait(eng, b.w)
            for ev in b.r.values():
                if ev.eng is not eng:
                    s._wait(eng, ev)

    def _mark(s, ev, reads, writes):
        for b in writes:
            b.w = ev
            b.r = {}
        key = id(ev.eng) if ev.eng is not None else ("d", ev.sem.num)
        for b in reads:
            b.r[key] = ev

    def op(s, eng, fn, reads=(), writes=(), signal=True):
        s._deps(eng, reads, writes)
        ins = fn()
        if signal:
            ins.then_inc(eng.sem, 1)
            eng.cnt += 1
            ev = Ev(eng.sem, eng.cnt, eng)
        else:
            ev = Ev(eng.sem, eng.cnt + 1, eng)
        s._mark(ev, reads, writes)
        return ev

    def dma(s, q, out, in_, reads=(), writes=()):
        s._deps(q, reads, writes)
        i = s.di
        s.di = (s.di + 1) % NDS
        s._wait(q, s.dlast[i])
        q.h.dma_start(out=out, in_=in_).then_inc(s.dsems[i], 16)
        s.dval[i] += 16
        ev = Ev(s.dsems[i], s.dval[i], None)
        s.dlast[i] = ev
        s._mark(ev, reads, writes)
        return ev

    def A(s, fn, r=(), w=()):
        return s.op(s.act, fn, r, w)

    def V(s, fn, r=(), w=()):
        return s.op(s.dve, fn, r, w)

    def bank(s):
        i = s.nbank
        s.nbank = (s.nbank + 1) % 8
        return s.banks[i], s.bb[i]

    def build(s):
        nc = s.nc
        L = s.layers
        dt = lambda name, shape, dtype=F32, kind="ExternalInput": nc.dram_tensor(name, shape, dtype, kind=kind).ap()
        s.x_in = dt("x_in", [NTILE * 128, D])
        s.out = dt("out", [16 * 128, D], kind="ExternalOutput")
        s.cst_f = dt("cst_f", [128, 384])
        s.cst_bf = dt("cst_bf", [128, 256], BF16)
        s.post_rep = dt("post_rep", [4, 128, 2 * D])
        s.W = {}
        kinds = sorted(set(l % 3 for l in L))
        if 0 in kinds:
            s.W["a_w_in"] = dt("a_w_in", [2, D, 3 * E])
            s.W["a_w_out"] = dt("a_w_out", [2, E, D])
            s.W["a_wsT"] = dt("a_wsT", [2, 128, 1024])
            s.W["a_bs_rep"] = dt("a_bs_rep", [2, 128, 1024])
            s.W["a_vec"] = dt("a_vec", [2, 128, 64])
        if 1 in kinds:
            s.W["b_w_in"] = dt("b_w_in", [1, D, 2 * E])
            s.W["b_w_pool"] = dt("b_w_pool", [1, 4, 1024, 1024])
            s.W["b_w_out"] = dt("b_w_out", [1, E, D])
            s.W["b_vec"] = dt("b_vec", [1, 128, 32])
        if 2 in kinds:
            s.W["c_w_in"] = dt("c_w_in", [1, D, 3 * E])
            s.W["c_w_out"] = dt("c_w_out", [1, E, D])
            s.W["c_vec"] = dt("c_vec", [1, 128, 32 * 31 + 96])
        s.Xs = nc.dram_tensor("xs_scr", [NTILE * 128, D], F32, kind="Internal").ap()
        s.Hs = nc.dram_tensor("hs_scr", [NTILE * 128, D], F32, kind="Internal").ap()
        s.Xb = [Buf(f"X{j}") for j in range(NTILE)]
        s.Hb = [Buf(f"H{j}") for j in range(NTILE)]

        s.banks = [nc.alloc_psum_tensor(f"ps{i}", [128, 512], F32) for i in range(8)]
        s.bb = [Buf(f"bank{i}") for i in range(8)]
        s.hT = nc.alloc_sbuf_tensor("hT", [128, 16, TMAX], BF16)
        s.HTb = [Buf(f"hT{j}") for j in range(9)]
        s.G = nc.alloc_sbuf_tensor("G", [128, 9 * 4096], BF16)
        s.Gb = [[Buf(f"G{c}_{j}") for j in range(9)] for c in range(32)]
        s.GbT = [[s.Gb[c][j] for c in range(32)] for j in range(9)]
        s.WA = nc.alloc_sbuf_tensor("WA", [128, NUNIT * UNIT], BF16)
        s.Wb = [Buf(f"wa{i}") for i in range(NUNIT)]
        s.wptr = 0
        s.wown = [None] * NUNIT
        s.cur_tok = None
        s.LX = nc.alloc_sbuf_tensor("LX", [128, LXB // 2], BF16)
        s.lx_live = []
        s.vtail = nc.alloc_sbuf_tensor("vtail", [128, 32, 16], F32)
        s.vtb = [Buf() for _ in range(32)]
        s.gtail = nc.alloc_sbuf_tensor("gtail", [128, 32, 32], BF16)
        s.gtb = [Buf() for _ in range(32)]
        s.cf = nc.alloc_sbuf_tensor("cf", [128, 384], F32)
        s.cb = nc.alloc_sbuf_tensor("cb", [128, 256], BF16)
        s.cfb = Buf("cf")
        s.cbb = Buf("cb")
        s.sm = nc.alloc_sbuf_tensor("sm", [128, 2, 32], F32)
        s.smb = [Buf(), Buf()]

        s.dma(s.sp, s.cf[:, :], s.cst_f[:, :], (), (s.cfb,))
        s.dma(s.sp, s.cb[:, :], s.cst_bf[:, :], (), (s.cbb,))
        s.identb = s.cb[:, 0:128]
        s.onesb = s.cb[:, 128:256]
        s.tri = s.cf[:, 0:128]
        s.onesf = s.cf[:, 128:256]
        s.mask = s.cf[:, 256:257]
        s.invc = s.cf[:, 257:321]

        S = []
        for p, (g0, g1) in enumerate(PASSES):
            P = (p, g0, g1 - g0)
            for li, l in enumerate(L):
                is_last = li == len(L) - 1
                kind = l % 3
                out_needed = (p == 0) and any((m % 3) in (1, 2) for m in L[li + 1:])
                lo5 = 0 if (p != 0 or out_needed) else 1
                lo = 0 if (p != 0 or kind in (1, 2) or out_needed) else 1
                if li == 0:
                    S += s.phase0(P, lo)
                if kind == 0:
                    st, xp = s.layerA(P, l, lo)
                elif kind == 1:
                    st, xp = s.layerB(P, l, lo5)
                else:
                    st, xp = s.layerC(P, l, lo5)
                S += st
                S += s.phase45(P, l, lo5, is_last, xp, li == 0)
        s.run_stages(S)
        for ev in s.out_events:
            s._wait(s.sp, ev)
        return nc

    @staticmethod
    def tbs(lo, n):
        m = n - lo
        nb = (m + 3) // 4
        out = []
        j = lo
        for i in range(nb):
            k = m // nb + (1 if i < m % nb else 0)
            out.append((j, k))
            j += k
        return out

    def lx_begin(s):
        s.lx_off = 0
        s.lx_cur = []
        return s.lx_cur

    def lx_activate(s, bufs):
        fence = {}
        for b in s.lx_live:
            evs = list(b.r.values())
            if b.w is not None:
                evs.append(b.w)
            for ev in evs:
                key = id(ev.eng) if ev.eng is not None else ("d", ev.sem.num)
                if key not in fence or fence[key].val < ev.val:
                    fence[key] = ev
        for b in bufs:
            b.w = None
            b.r = dict(fence)
        s.lx_live = bufs

    def lx(s, nbytes, dtype=F32, name=""):
        nb = (nbytes + 31) // 32 * 32
        assert s.lx_off + nb <= LXB, (s.lx_off, nb, name)
        ap = s.LX[:, s.lx_off // 2:(s.lx_off + nbytes) // 2]
        s.lx_off += nb
        if dtype == F32:
            ap = ap.bitcast(F32)
        b = Buf(name)
        s.lx_cur.append(b)
        return ap, b

    def walloc(s, n):
        for attempt in range(3):
            if s.wptr + n > NUNIT:
                s.wptr = 0
            u = s.wptr
            bad = [i for i in range(u, u + n) if s.wown[i] is not None and not s.wown[i][0]]
            if not bad:
                break
            s.wptr = bad[-1] + 1
        else:
            raise RuntimeError("weight arena: no free units")
        for i in range(u, u + n):
            s.wown[i] = s.cur_tok
        s.wptr = u + n
        return u

    def wload(s, dram_ap, k, cols, nunits):
        assert k * cols == nunits * UNIT
        u = s.walloc(nunits)
        bufs = s.Wb[u:u + nunits]
        slot = s.WA[:, u * UNIT:(u + nunits) * UNIT].rearrange("p (k c) -> p k c", k=k)
        s.dma(s.pool, slot, dram_ap, (), bufs)
        return slot, bufs

    def run_stages(s, stages, look=2):
        ctxs = {}
        toks = {}
        n = len(stages)

        def ld(i):
            toks[i] = s.cur_tok = [False]
            ctxs[i] = stages[i][0]()
        for i in range(min(look, n)):
            ld(i)
        for i in range(n):
            stages[i][1](ctxs.pop(i))
            toks[i][0] = True
            if i + look < n:
                ld(i + look)

    def Gc(s, c, j0, k):
        o = j0 * 4096 + c * 128
        return AP3(s.G[:, o:o + 1], [[4096, k], [1, 128]])

    def yT(s, c, jj):
        o = jj * 4096 + c * 128
        return s.G[:, o:o + 128]

    def inproj(s, slot, sbufs, q, tb):
        j0, k = tb
        ntok = k * 128
        bk, bb = s.bank()
        hb = s.HTb[j0:j0 + k]
        for dc in range(16):
            s.op(s.pe, lambda dc=dc: s.nc.tensor.matmul(
                bk[:, 0:ntok], slot[:, dc, q * 128:(q + 1) * 128], s.hT[:, dc, j0 * 128:j0 * 128 + ntok],
                start=(dc == 0), stop=(dc == 15)),
                reads=list(sbufs) + hb, writes=[bb], signal=(dc == 15))
        return bk, bb

    def transposes(s, src, srcb, jj, dve_only=False):
        nc = s.nc
        for half in range(2):
            bk, bb = s.bank()
            bkb = bk[:, :].bitcast(BF16)
            for k in range(8):
                dc = half * 8 + k
                s.op(s.pe, lambda k=k, dc=dc: nc.tensor.transpose(
                    bkb[:, k * 128:(k + 1) * 128], src[:, dc * 128:(dc + 1) * 128], s.identb),
                    reads=list(srcb) + [s.cbb], writes=[bb], signal=(k == 7))
            dst = s.hT[:, half * 8:(half + 1) * 8, jj * 128:(jj + 1) * 128]
            srcv = bkb.rearrange("p (k t) -> p k t", k=8)
            if half == 0 and not dve_only:
                s.A(lambda: nc.scalar.copy(out=dst, in_=srcv), [bb], [s.HTb[jj]])
            else:
                s.V(lambda: nc.vector.tensor_copy(out=dst, in_=srcv), [bb], [s.HTb[jj]])

    def p5_bufs(s):
        f = lambda j: s.G[:, j * 4096:(j + 1) * 4096].bitcast(F32)
        return [(f(0), s.GbT[0], f(1), s.GbT[1]), (f(2), s.GbT[2], f(3), s.GbT[3]), (f(4), s.GbT[4], f(5), s.GbT[5])]

    def phase0(s, P, lo):
        p, g0, n = P
        stages = []
        for jj in range(lo, n):
            i = jj % 6
            hb = s.G[:, 6 * 4096 + i * 2048:6 * 4096 + (i + 1) * 2048]
            hbb = [s.Gb[c][6 + i // 2] for c in range((i % 2) * 16, (i % 2) * 16 + 16)]

            def loads(jj=jj, hb=hb, hbb=hbb):
                s.dma(s.pool, hb, s.x_in[(g0 + jj) * 128:(g0 + jj + 1) * 128, :], (), hbb)

            def compute(ctx, jj=jj, hb=hb, hbb=hbb):
                s.transposes(hb, hbb, jj)
            stages.append((loads, compute))
        return stages

    def phase45(s, P, l, lo5, is_last, xp, first_layer):
        nc = s.nc
        p, g0, n = P
        kind, j = l % 3, l // 3
        wout = s.W["abc"[kind] + "_w_out"][j].rearrange("(ec p) d -> p ec d", p=128)
        xpi = [0]
        xdone = {}
        src = s.x_in if first_layer else s.Hs
        pieces = [(b, jj) for b in range(8) for jj in range(lo5, n)]
        NXP = len(xp)

        def hload(i):
            if i >= len(pieces):
                return
            b, jj = pieces[i]
            gt = g0 + jj
            x_, xb_ = xp[i % NXP]
            s.dma(s.sp, x_, src[gt * 128:(gt + 1) * 128, b * 256:(b + 1) * 256], (s.Hb[gt],) if src is s.Hs else (), (xb_,))

        def p4_stage(b):
            def loads():
                return s.wload(wout[:, :, b * 256:(b + 1) * 256], 32, 256, 2)

            def compute(ctx):
                slot, sb = ctx
                for jj in range(lo5, n):
                    gt = g0 + jj
                    i = xpi[0]
                    assert pieces[i] == (b, jj)
                    if i == 0:
                        for k in range(NXP):
                            hload(k)
                        load_post()
                    bk, bb = s.bank()
                    for ec in range(32):
                        s.op(s.pe, lambda ec=ec: nc.tensor.matmul(
                            bk[:, 0:256], s.yT(ec, jj), slot[:, ec, :], start=(ec == 0), stop=(ec == 31)),
                            reads=list(sb) + [s.Gb[ec][jj]], writes=[bb], signal=(ec == 31))
                    x_, xb_ = xp[i % NXP]
                    xpi[0] += 1
                    s.V(lambda: nc.vector.scalar_tensor_tensor(out=x_, in0=x_, scalar=ALPHA, in1=bk[:, 0:256],
                                                               op0=ALU.mult, op1=ALU.add), [bb, xb_], [xb_])
                    s.dma(s.act, s.Xs[gt * 128:(gt + 1) * 128, b * 256:(b + 1) * 256], x_, (xb_,), (s.Xb[gt],))
                    xdone[gt] = xdone.get(gt, 0) + 1
                    hload(i + NXP)
                    if b == 7:
                        after_tile(jj)
            return (loads, compute)

        pb = s.p5_bufs()
        post = s.LX[:, 2048:2048 + 8192].bitcast(F32)
        pbuf = Buf("post")
        postb = [pbuf]
        hbs = [s.LX[:, 10240 + i * 2048:10240 + (i + 1) * 2048] for i in range(6)]
        hbb = [Buf(f"hb{i}") for i in range(6)]
        post_loaded = [False]
        pend = [None]

        def load_post():
            if not post_loaded[0]:
                fence = {}
                for b_ in s.lx_live:
                    for ev in list(b_.r.values()) + ([b_.w] if b_.w is not None else []):
                        key = id(ev.eng) if ev.eng is not None else ("d", ev.sem.num)
                        if key not in fence or fence[key].val < ev.val:
                            fence[key] = ev
                for nb_ in [pbuf] + hbb:
                    nb_.r = dict(fence)
                s.lx_live = list(s.lx_live) + [pbuf] + hbb
                s.dma(s.sp, post, s.post_rep[l], (), postb)
                post_loaded[0] = True

        def p5_loads(jj):
            gt = g0 + jj
            XA, XAb, HA, HAb = pb[jj % 3]
            load_post()
            assert xdone.get(gt, 0) == 8, (gt, xdone.get(gt, 0))
            s.dma(s.sp, XA, s.Xs[gt * 128:(gt + 1) * 128, :], (s.Xb[gt],), XAb)

        def p5_front(jj):
            gt = g0 + jj
            XA, XAb, HA, HAb = pb[jj % 3]
            sm = s.sm[:, jj % 2, :]
            smb = s.smb[jj % 2]
            st = sm[:, 0:24].rearrange("p (k s) -> p k s", k=4)
            for k in range(4):
                s.V(lambda k=k: nc.vector.bn_stats(out=st[:, k, :], in_=XA[:, k * 512:(k + 1) * 512]), XAb, [smb])
            mv = sm[:, 24:26]
            rs = sm[:, 26:27]
            nmr = sm[:, 27:28]
            s.V(lambda: nc.vector.bn_aggr(out=mv, in_=sm[:, 0:24]), [smb], [smb])
            s.V(lambda: nc.vector.tensor_scalar(out=rs, in0=mv[:, 1:2], scalar1=EPS, scalar2=None, op0=ALU.add), [smb], [smb])
            s.A(lambda: nc.scalar.activation(out=rs, in_=rs, func=AF.Sqrt), [smb], [smb])
            s.V(lambda: nc.vector.reciprocal(out=rs, in_=rs), [smb], [smb])
            s.V(lambda: nc.vector.tensor_scalar(out=nmr, in0=mv[:, 0:1], scalar1=rs, scalar2=-1.0, op0=ALU.mult, op1=ALU.mult),
                [smb], [smb])
            s.A(lambda: nc.scalar.activation(out=HA, in_=XA, func=AF.Identity, bias=nmr, scale=rs), XAb + [smb], HAb)
            s.V(lambda: nc.vector.tensor_tensor(out=HA, in0=HA, in1=post[:, 0:D], op=ALU.mult), HAb + postb, HAb)
            s.op(s.pool, lambda: nc.gpsimd.tensor_tensor(out=XA, in0=HA, in1=post[:, D:2 * D], op=ALU.add), HAb + postb, XAb)
            if is_last:
                ev = s.dma(s.pool, s.out[(gt - 1) * 128:gt * 128, :], XA, XAb, ())
                s.out_events.append(ev)
            else:
                s.dma(s.pool, s.Hs[gt * 128:(gt + 1) * 128, :], XA, XAb, (s.Hb[gt],))

        def run_pend():
            if pend[0] is not None:
                f = pend[0]
                pend[0] = None
                f()

        started = []
        loaded = []
        cast_done = set()
        front_done = set()
        next_t = [lo5]
        pcast = [None]

        def run_pcast():
            if pcast[0] is not None:
                f = pcast[0]
                pcast[0] = None
                f()

        def mk_cast(jj, slot):
            def f():
                XA, XAb, HA, HAb = pb[jj % 3]
                s.A(lambda: nc.scalar.copy(out=hbs[slot], in_=XA), XAb, [hbb[slot]])
                cast_done.add(jj)
            return f

        tiles = list(range(lo5, n))
        lset = set()
        efront = []

        def can_load(u):
            if u in lset or u >= n:
                return False
            if u - 3 < lo5:
                return True
            return (u - 3) in (front_done if is_last else cast_done)

        def after_tile(jj):
            while next_t[0] < n and len(lset) < 3:
                t = next_t[0]
                if jj < max(t, 2 * (t % 3) + 1) or not can_load(t):
                    break
                p5_loads(t)
                lset.add(t)
                next_t[0] += 1
            k = len(efront)
            if k < 2 and k < len(tiles) and jj == n - 3 + k and tiles[k] in lset:
                p5_front(tiles[k])
                front_done.add(tiles[k])
                efront.append(tiles[k])

        def flushT(ctx):
            prev = None
            for u in tiles[:2]:
                if can_load(u):
                    p5_loads(u)
                    lset.add(u)
            for t in tiles:
                if t not in lset:
                    assert can_load(t), t
                    p5_loads(t)
                    lset.add(t)
                if t not in efront:
                    p5_front(t)
                    front_done.add(t)
                if not is_last:
                    if prev is not None:
                        mk_cast(prev, (prev - lo5) % 6)()
                        s.transposes(hbs[(prev - lo5) % 6], [hbb[(prev - lo5) % 6]], prev, dve_only=True)
                    prev = t
                for u in (t + 1, t + 2):
                    if can_load(u):
                        p5_loads(u)
                        lset.add(u)
            if not is_last and prev is not None:
                mk_cast(prev, (prev - lo5) % 6)()
                s.transposes(hbs[(prev - lo5) % 6], [hbb[(prev - lo5) % 6]], prev, dve_only=True)

        stages = [p4_stage(b) for b in range(8)]
        stages.append((lambda: None, flushT))
        return stages

    def layerA(s, P, l, lo):
        nc = s.nc
        p, g0, n = P
        j = l // 3
        win = s.W["a_w_in"][j].rearrange("(dc p) c -> p dc c", p=128)
        bufs = s.lx_begin()
        xp = [s.lx(1024, F32, f"xp{i}") for i in range(4)]
        wsTb, wsTbb = s.lx(2048, BF16, "wsT")
        wsum, wsumb = s.lx(4096, F32, "wsum")
        bsr, bsrb = s.lx(4096, F32, "bsr")
        vec, vecb = s.lx(256, F32, "vecA")
        stats, statsb = s.lx(9 * 48 * 4, F32, "stats")
        mvs, mvsb = s.lx(9 * 4 * 4, F32, "mvs")
        Bc = [s.lx(512, F32, f"Bc{i}") for i in range(4)]
        scr = [s.lx(2048, F32, f"scr{i}") for i in range(6)]
        wtmp, wtmpb = s.lx(4096, F32, "wtmp")
        lng = vec[:, 0:32]
        lnb = vec[:, 32:64]

        def setup(ctx):
            s.lx_activate(bufs)
            s.dma(s.sp, wtmp, s.W["a_wsT"][j], (), (wtmpb,))
            s.dma(s.sp, bsr, s.W["a_bs_rep"][j], (), (bsrb,))
            s.dma(s.sp, vec, s.W["a_vec"][j], (), (vecb,))
            w3 = wtmp.rearrange("p (h t) -> p h t", h=8)
            s.V(lambda: nc.vector.tensor_tensor(out=w3, in0=w3, in1=AP3(s.tri, [[0, 8], [1, 128]]), op=ALU.mult),
                [wtmpb, s.cfb], [wtmpb])
            s.V(lambda: nc.vector.tensor_copy(out=wsTb, in_=wtmp), [wtmpb], [wsTbb])
            for hh in range(2):
                bk, bb = s.bank()
                s.op(s.pe, lambda hh=hh: nc.tensor.matmul(bk[:, :], s.onesf, wtmp[:, hh * 512:(hh + 1) * 512], start=True, stop=True),
                     reads=[wtmpb, s.cfb], writes=[bb])
                s.V(lambda hh=hh: nc.vector.tensor_copy(out=wsum[:, hh * 512:(hh + 1) * 512], in_=bk[:, :]), [bb], [wsumb])

        stages = [(lambda: None, setup)]

        def Gtm(hd, jj):
            o = jj * 4096 + hd * 512
            return s.G[:, o:o + 512]

        for hd in range(8):
            def loads(hd=hd):
                return s.wload(win[:, :, E + hd * 512:E + (hd + 1) * 512], 16, 512, 2)

            def compute(ctx, hd=hd):
                slot, sb = ctx
                for jj in range(lo, n):
                    bk, bb = s.bank()
                    for dc in range(16):
                        s.op(s.pe, lambda dc=dc: nc.tensor.matmul(
                            bk[:, :], s.hT[:, dc, jj * 128:(jj + 1) * 128], slot[:, dc, :], start=(dc == 0), stop=(dc == 15)),
                            reads=list(sb) + [s.HTb[jj]], writes=[bb], signal=(dc == 15))
                    gb = [s.Gb[hd * 4 + q][jj] for q in range(4)]
                    s.A(lambda: nc.scalar.activation(out=Gtm(hd, jj), in_=bk[:, :], func=AF.Gelu_apprx_tanh), [bb], gb)
                    so = (jj * 8 + hd) * 6
                    s.V(lambda: nc.vector.bn_stats(out=stats[:, so:so + 6], in_=Gtm(hd, jj)), gb, [statsb])
            stages.append((loads, compute))

        def p2(ctx):
            for jj in range(lo, n):
                mv = mvs[:, jj * 4:jj * 4 + 2]
                rs = mvs[:, jj * 4 + 2:jj * 4 + 3]
                s.V(lambda: nc.vector.bn_aggr(out=mv, in_=stats[:, jj * 48:(jj + 1) * 48]), [statsb], [mvsb])
                s.V(lambda: nc.vector.tensor_scalar(out=rs, in0=mv[:, 1:2], scalar1=EPS, scalar2=None, op0=ALU.add), [mvsb], [mvsb])
                s.A(lambda: nc.scalar.activation(out=rs, in_=rs, func=AF.Sqrt), [mvsb], [mvsb])
                s.V(lambda: nc.vector.reciprocal(out=rs, in_=rs), [mvsb], [mvsb])
                gv = s.G[:, jj * 4096:(jj + 1) * 4096]
                gb = s.GbT[jj]
                s.V(lambda: nc.vector.tensor_scalar(out=gv, in0=gv, scalar1=mv[:, 0:1], scalar2=rs,
                                                    op0=ALU.subtract, op1=ALU.mult), gb + [mvsb], gb)
        stages.append((lambda: None, p2))

        unit = [0]
        for cp in range(16):
            def loads(cp=cp):
                c0 = cp * 2
                us = s.wload(win[:, :, c0 * 128:c0 * 128 + 256], 16, 256, 1)
                zs = s.wload(win[:, :, 2 * E + c0 * 128:2 * E + c0 * 128 + 256], 16, 256, 1)
                return us, zs

            def compute(ctx, cp=cp):
                (us, usb), (zs, zsb) = ctx
                for q in range(2):
                    c = cp * 2 + q
                    hd, qq = c // 4, c % 4
                    bc, bcb = Bc[c % 4]
                    s.V(lambda: nc.vector.scalar_tensor_tensor(
                        out=bc, in0=wsum[:, hd * 128:(hd + 1) * 128], scalar=lnb[:, c:c + 1],
                        in1=bsr[:, hd * 128:(hd + 1) * 128], op0=ALU.mult, op1=ALU.add),
                        [wsumb, bsrb, vecb], [bcb])
                    for tb in s.tbs(lo, n):
                        j0, k = tb
                        ntok = k * 128
                        bu, bub = s.inproj(us, usb, q, tb)
                        bz, bzb = s.inproj(zs, zsb, q, tb)
                        bm, bmb = s.bank()
                        for i in range(k):
                            jj = j0 + i
                            s.op(s.pe, lambda i=i, jj=jj: nc.tensor.matmul(
                                bm[:, i * 128:(i + 1) * 128], s.yT(c, jj),
                                wsTb[:, hd * 128:(hd + 1) * 128], start=True, stop=True),
                                reads=[s.Gb[c][jj], wsTbb], writes=[bmb], signal=(i == k - 1))
                        u0 = (unit[0] % 2) * 3
                        unit[0] += 1
                        (s1, s1b), (s2, s2b), (s3, s3b) = scr[u0], scr[u0 + 1], scr[u0 + 2]
                        r3 = lambda a: a[:, 0:ntok].rearrange("p (a t) -> p a t", a=k)
                        s.A(lambda: nc.scalar.activation(out=s1[:, 0:ntok], in_=bu[:, 0:ntok], func=AF.Gelu_apprx_tanh), [bub], [s1b])
                        s.A(lambda: nc.scalar.activation(out=s2[:, 0:ntok], in_=bz[:, 0:ntok], func=AF.Silu), [bzb], [s2b])
                        s.V(lambda: nc.vector.scalar_tensor_tensor(
                            out=r3(s3), in0=r3(bm), scalar=lng[:, c:c + 1],
                            in1=AP3(bc, [[0, k], [1, 128]]), op0=ALU.mult, op1=ALU.add),
                            [bmb, bcb, vecb], [s3b])
                        s.V(lambda: nc.vector.tensor_tensor(out=s1[:, 0:ntok], in0=s1[:, 0:ntok], in1=s3[:, 0:ntok], op=ALU.mult),
                            [s1b, s3b], [s1b])
                        gb = [s.Gb[c][j0 + i] for i in range(k)]
                        s.V(lambda: nc.vector.tensor_tensor(out=s.Gc(c, j0, k), in0=r3(s1), in1=r3(s2), op=ALU.mult),
                            [s1b, s2b], gb)
            stages.append((loads, compute))
        return stages, [(a[:, 0:256], b) for a, b in xp]

    def layerB(s, P, l, lo5):
        nc = s.nc
        p, g0, n = P
        T = n * 128
        p0 = p == 0
        win = s.W["b_w_in"][0].rearrange("(dc p) c -> p dc c", p=128)
        bufs = s.lx_begin()
        xp = [s.lx(1024, F32, f"xp{i}") for i in range(4)]
        pT, _ = s.lx(8 * TMAX * 2, BF16, "pT")
        pT3 = pT.rearrange("p (c t) -> p c t", c=8)
        pTbs = [Buf() for _ in range(8)]
        bufs += pTbs
        PADL = 16 + TMAX
        vp = [s.lx(PADL * 4, F32, f"vp{i}") for i in range(2)]
        wa, wab = s.lx(PADL * 4, F32, "wa")
        wb_, wbb = s.lx(PADL * 4, F32, "wb")
        vec, vecb = s.lx(128, F32, "vecB")
        fix, fixb = s.lx(64, F32, "fix")
        scr = [s.lx(2048, F32, f"scr{i}") for i in range(2)]
        L = 16 + T

        def setup(ctx):
            s.lx_activate(bufs)
            s.dma(s.sp, vec, s.W["b_vec"][0], (), (vecb,))
        stages = [(lambda: None, setup)]
        cnt = [0]
        for g in range(4):
            w = 2 ** (g + 1)
            for sp_ in range(4):
                def loads(g=g, sp_=sp_):
                    c0 = g * 8 + sp_ * 2
                    return s.wload(win[:, :, c0 * 128:c0 * 128 + 256], 16, 256, 1)

                def compute(ctx, g=g, sp_=sp_, w=w):
                    slot, sb = ctx
                    for q in range(2):
                        c = g * 8 + sp_ * 2 + q
                        cc = c - g * 8
                        v, vb = vp[cnt[0] % 2]
                        cnt[0] += 1
                        if p0:
                            s.V(lambda: nc.vector.memset(v[:, 0:16], 0.0), [], [vb])
                        else:
                            s.V(lambda: nc.vector.tensor_copy(out=v[:, 0:16], in_=s.vtail[:, c, :]), [s.vtb[c]], [vb])
                        for tb in s.tbs(0, n):
                            j0, k = tb
                            bk, bb = s.inproj(slot, sb, q, tb)
                            s.A(lambda: nc.scalar.copy(out=v[:, 16 + j0 * 128:16 + (j0 + k) * 128], in_=bk[:, 0:k * 128]), [bb], [vb])
                        if p0:
                            s.V(lambda: nc.vector.tensor_scalar(out=v[:, 16:144], in0=v[:, 16:144], scalar1=s.mask, scalar2=None,
                                                                op0=ALU.mult), [vb, s.cfb], [vb])
                            s.V(lambda: nc.vector.tensor_copy(out=s.vtail[:, c, :], in_=v[:, L - 16:L]), [vb], [s.vtb[c]])
                        cur, curb = v, vb
                        m = 1
                        while m < w:
                            o, ob = (wa, wab) if cur is not wa else (wb_, wbb)
                            s.V(lambda cur=cur, o=o, m=m: nc.vector.tensor_tensor(
                                out=o[:, 2 * m - 1:L], in0=cur[:, 2 * m - 1:L], in1=cur[:, m - 1:L - m], op=ALU.add),
                                [curb], [ob])
                            cur, curb = o, ob
                            m *= 2
                        s.V(lambda cur=cur: nc.vector.scalar_tensor_tensor(
                            out=pT3[:, cc, 0:T], in0=cur[:, 16:L], scalar=1.0 / w, in1=v[:, 16:L],
                            op0=ALU.mult, op1=ALU.subtract), [curb, vb], [pTbs[cc]])
                        if p0:
                            s.V(lambda cur=cur: nc.vector.tensor_tensor(out=fix[:, 0:16], in0=cur[:, 144:160],
                                                                        in1=s.invc[:, g * 16:(g + 1) * 16], op=ALU.mult),
                                [curb, s.cfb], [fixb])
                            s.V(lambda: nc.vector.tensor_tensor(out=pT3[:, cc, 128:144], in0=fix[:, 0:16], in1=v[:, 144:160],
                                                                op=ALU.subtract), [fixb, vb], [pTbs[cc]])
                stages.append((loads, compute))
            for dq in range(2):
                def loads(g=g, dq=dq):
                    e0 = g * 1024 + dq * 512
                    wp = s.W["b_w_pool"][0, g].rearrange("(cc p) d -> p cc d", p=128)
                    a = s.wload(wp[:, :, dq * 512:(dq + 1) * 512], 8, 512, 1)
                    z0 = s.wload(win[:, :, E + e0:E + e0 + 256], 16, 256, 1)
                    z1 = s.wload(win[:, :, E + e0 + 256:E + e0 + 512], 16, 256, 1)
                    return a, z0, z1

                def compute(ctx, g=g, dq=dq):
                    (wp, wpb), z0, z1 = ctx
                    for dch in range(4):
                        c = g * 8 + dq * 4 + dch
                        zs, zsb = z0 if dch < 2 else z1
                        for tb in s.tbs(lo5, n):
                            j0, k = tb
                            ntok = k * 128
                            bq, bqb = s.bank()
                            for cc in range(8):
                                s.op(s.pe, lambda cc=cc: nc.tensor.matmul(
                                    bq[:, 0:ntok], wp[:, cc, dch * 128:(dch + 1) * 128], pT3[:, cc, j0 * 128:j0 * 128 + ntok],
                                    start=(cc == 0), stop=(cc == 7)),
                                    reads=list(wpb) + [pTbs[cc]], writes=[bqb], signal=(cc == 7))
                            bz, bzb = s.inproj(zs, zsb, dch % 2, tb)
                            sz, szb = scr[cnt[0] % 2]
                            cnt[0] += 1
                            r3 = lambda a: a[:, 0:ntok].rearrange("p (a t) -> p a t", a=k)
                            s.A(lambda: nc.scalar.activation(out=sz[:, 0:ntok], in_=bz[:, 0:ntok], func=AF.Silu), [bzb], [szb])
                            gb = [s.Gb[c][j0 + i] for i in range(k)]
                            s.V(lambda: nc.vector.scalar_tensor_tensor(
                                out=s.Gc(c, j0, k), in0=r3(bq), scalar=vec[:, c:c + 1], in1=r3(sz),
                                op0=ALU.mult, op1=ALU.mult), [bqb, szb, vecb], gb)
                stages.append((loads, compute))
        return stages, [(a[:, 0:256], b) for a, b in xp]

    def layerC(s, P, l, lo5):
        nc = s.nc
        p, g0, n = P
        T = n * 128
        p0 = p == 0
        win = s.W["c_w_in"][0].rearrange("(dc p) c -> p dc c", p=128)
        bufs = s.lx_begin()
        xp = [s.lx(1024, F32, f"xp{i}") for i in range(3)]
        PADL = 32 + TMAX
        gp = [s.lx(PADL * 2, BF16, f"gp{i}") for i in range(2)]
        dg = [s.lx(31 * 128 * 2, BF16, f"dg{i}") for i in range(2)]
        MU, MUb = s.lx(TMAX * 4, F32, "MU")
        RS, RSb = s.lx(TMAX * 4, F32, "RS")
        vec, vecb = s.lx((32 * 31 + 96) * 4, F32, "vecC")
        sq = [s.lx(1024, BF16, f"sq{i}") for i in range(2)]
        scr = [s.lx(2048, F32, f"scr{i}") for i in range(3)]
        cw = vec[:, 0:32 * 31]
        convb = vec[:, 32 * 31:32 * 31 + 32]
        lng = vec[:, 32 * 31 + 32:32 * 31 + 64]
        lnb = vec[:, 32 * 31 + 64:32 * 31 + 96]
        L = 32 + T

        def setup(ctx):
            s.lx_activate(bufs)
            s.dma(s.sp, vec, s.W["c_vec"][0], (), (vecb,))
        stages = [(lambda: None, setup)]
        cnt = [0]
        prev = [None]

        NDV = 8
        NPE = 31 - NDV

        def conv(c):
            g_, gb_ = gp[c % 2]
            d3, db = dg[c % 2]
            d3 = d3.rearrange("p (k e) -> p k e", k=31)
            for tb in s.tbs(0, n):
                j0, k = tb
                ntok = k * 128
                bc, bcb = s.bank()
                for kk in range(NPE):
                    st = 32 + j0 * 128 - 30 + kk
                    s.op(s.pe, lambda kk=kk, st=st: nc.tensor.matmul(
                        bc[:, 0:ntok], d3[:, kk, :], g_[:, st:st + ntok], start=(kk == 0), stop=(kk == NPE - 1)),
                        reads=[db, gb_], writes=[bcb], signal=(kk == NPE - 1))
                sc, scb = scr[cnt[0] % 3]
                cnt[0] += 1
                for i, kk in enumerate(range(NPE, 31)):
                    st = 32 + j0 * 128 - 30 + kk
                    src_ = bc[:, 0:ntok] if i == 0 else sc[:, 0:ntok]
                    s.V(lambda kk=kk, st=st, src_=src_: nc.vector.scalar_tensor_tensor(
                        out=sc[:, 0:ntok], in0=g_[:, st:st + ntok], scalar=cw[:, c * 31 + kk:c * 31 + kk + 1], in1=src_,
                        op0=ALU.mult, op1=ALU.add), [gb_, vecb] + ([bcb] if i == 0 else [scb]), [scb])
                gb = [s.Gb[c][j0 + i] for i in range(k)]
                s.A(lambda: nc.scalar.activation(out=s.Gc(c, j0, k), in_=sc[:, 0:ntok].rearrange("p (a t) -> p a t", a=k),
                                                 func=AF.Identity, bias=convb[:, c:c + 1], scale=1.0), [scb, vecb], gb)

        for cp in range(16):
            def loads(cp=cp):
                c0 = cp * 2
                a = s.wload(win[:, :, c0 * 128:c0 * 128 + 256], 16, 256, 1)
                g = s.wload(win[:, :, E + c0 * 128:E + c0 * 128 + 256], 16, 256, 1)
                return a, g

            def compute(ctx, cp=cp):
                (as_, asb), (gs, gsb) = ctx
                for q in range(2):
                    c = cp * 2 + q
                    g_, gb_ = gp[c % 2]
                    d_, db = dg[c % 2]
                    if p0:
                        s.V(lambda: nc.vector.memset(g_[:, 0:32], 0.0), [], [gb_])
                    else:
                        s.V(lambda: nc.vector.tensor_copy(out=g_[:, 0:32], in_=s.gtail[:, c, :]), [s.gtb[c]], [gb_])
                    s.V(lambda: nc.vector.tensor_tensor(
                        out=d_.rearrange("p (k e) -> p k e", k=31)[:, 0:NPE, :], in0=AP3(s.identb, [[0, NPE], [1, 128]]),
                        in1=AP3(cw[:, c * 31:c * 31 + 1], [[1, NPE], [0, 128]]), op=ALU.mult),
                        [s.cbb, vecb], [db])
                    for tb in s.tbs(0, n):
                        j0, k = tb
                        ntok = k * 128
                        ba, bab = s.inproj(as_, asb, q, tb)
                        bg, bgb = s.inproj(gs, gsb, q, tb)
                        sg, sgb = scr[cnt[0] % 3]
                        cnt[0] += 1
                        s.A(lambda: nc.scalar.activation(out=sg[:, 0:ntok], in_=bg[:, 0:ntok], func=AF.Sigmoid), [bgb], [sgb])
                        s.V(lambda: nc.vector.tensor_tensor(out=g_[:, 32 + j0 * 128:32 + j0 * 128 + ntok], in0=ba[:, 0:ntok],
                                                            in1=sg[:, 0:ntok], op=ALU.mult), [bab, sgb], [gb_])
                    if p0:
                        s.V(lambda: nc.vector.tensor_scalar(out=g_[:, 32:160], in0=g_[:, 32:160], scalar1=s.mask, scalar2=None,
                                                            op0=ALU.mult), [gb_, s.cfb], [gb_])
                        s.V(lambda: nc.vector.tensor_copy(out=s.gtail[:, c, :], in_=g_[:, L - 32:L]), [gb_], [s.gtb[c]])
                    if prev[0] is not None:
                        conv(prev[0])
                    prev[0] = c
            stages.append((loads, compute))

        def p2(ctx):
            conv(prev[0])
            for tb in s.tbs(lo5, n):
                j0, k = tb
                ntok = k * 128
                b1, b1b = s.bank()
                b2, b2b = s.bank()
                for c in range(32):
                    q_, qb = sq[c % 2]
                    gb = [s.Gb[c][j0 + i] for i in range(k)]
                    s.A(lambda: nc.scalar.activation(out=q_[:, 0:ntok].rearrange("p (a t) -> p a t", a=k), in_=s.Gc(c, j0, k),
                                                     func=AF.Square), gb, [qb])
                    s.op(s.pe, lambda: nc.tensor.matmul(b1[:, 0:ntok], s.onesb, s.Gc(c, j0, k), start=(c == 0), stop=(c == 31)),
                         reads=gb + [s.cbb], writes=[b1b], signal=(c == 31))
                    s.op(s.pe, lambda: nc.tensor.matmul(b2[:, 0:ntok], s.onesb, q_[:, 0:ntok], start=(c == 0), stop=(c == 31)),
                         reads=[qb, s.cbb], writes=[b2b], signal=True)
                mu = MU[:, j0 * 128:j0 * 128 + ntok]
                rs = RS[:, j0 * 128:j0 * 128 + ntok]
                t_, tb_ = scr[0]
                s.A(lambda: nc.scalar.activation(out=mu, in_=b1[:, 0:ntok], func=AF.Identity, scale=1.0 / E), [b1b], [MUb])
                s.V(lambda: nc.vector.tensor_tensor(out=t_[:, 0:ntok], in0=mu, in1=mu, op=ALU.mult), [MUb], [tb_])
                s.V(lambda: nc.vector.scalar_tensor_tensor(out=rs, in0=b2[:, 0:ntok], scalar=1.0 / E, in1=t_[:, 0:ntok],
                                                           op0=ALU.mult, op1=ALU.subtract), [b2b, tb_], [RSb])
                s.V(lambda: nc.vector.tensor_scalar(out=rs, in0=rs, scalar1=EPS, scalar2=None, op0=ALU.add), [RSb], [RSb])
                s.A(lambda: nc.scalar.activation(out=rs, in_=rs, func=AF.Sqrt), [RSb], [RSb])
                s.V(lambda: nc.vector.reciprocal(out=rs, in_=rs), [RSb], [RSb])
        stages.append((lambda: None, p2))

        for cp in range(16):
            def loads(cp=cp):
                c0 = cp * 2
                return s.wload(win[:, :, 2 * E + c0 * 128:2 * E + c0 * 128 + 256], 16, 256, 1)

            def compute(ctx, cp=cp):
                zs, zsb = ctx
                for q in range(2):
                    c = cp * 2 + q
                    for tb in s.tbs(lo5, n):
                        j0, k = tb
                        ntok = k * 128
                        bz, bzb = s.inproj(zs, zsb, q, tb)
                        i0 = cnt[0] % 3
                        cnt[0] += 1
                        (s1, s1b) = scr[i0]
                        (s2, s2b) = scr[(i0 + 1) % 3]
                        r3 = lambda a: a[:, 0:ntok].rearrange("p (a t) -> p a t", a=k)
                        gb = [s.Gb[c][j0 + i] for i in range(k)]
                        s.A(lambda: nc.scalar.activation(out=s1[:, 0:ntok], in_=bz[:, 0:ntok], func=AF.Silu), [bzb], [s1b])
                        s.V(lambda: nc.vector.tensor_tensor(out=r3(s2), in0=s.Gc(c, j0, k),
                                                            in1=r3(MU[:, j0 * 128:j0 * 128 + ntok]), op=ALU.subtract), gb + [MUb], [s2b])
                        s.V(lambda: nc.vector.tensor_tensor(out=s2[:, 0:ntok], in0=s2[:, 0:ntok],
                                                            in1=RS[:, j0 * 128:j0 * 128 + ntok], op=ALU.mult), [s2b, RSb], [s2b])
                        s.A(lambda: nc.scalar.activation(out=s2[:, 0:ntok], in_=s2[:, 0:ntok], func=AF.Silu,
                                                         bias=lnb[:, c:c + 1], scale=lng[:, c:c + 1]), [s2b, vecb], [s2b])
                        s.V(lambda: nc.vector.tensor_tensor(out=s.Gc(c, j0, k), in0=r3(s2), in1=r3(s1), op=ALU.mult),
                            [s1b, s2b], gb)
            stages.append((loads, compute))
        return stages, [(a[:, 0:256], b) for a, b in xp]


_PROGS = {}


def _get_prog(layers):
    key = tuple(layers)
    if key not in _PROGS:
        p = Prog(list(layers))
        p.build()
        _PROGS[key] = p
    return _PROGS[key]


def _consts():
    cb = np.zeros((128, 256), dtype=ml_dtypes.bfloat16)
    cb[:, 0:128] = np.eye(128, dtype=np.float32).astype(ml_dtypes.bfloat16)
    cb[:, 128:256] = np.ones((128, 128), dtype=np.float32).astype(ml_dtypes.bfloat16)
    cfs = []
    for core in range(NCORE):
        cf = np.zeros((128, 384), dtype=np.float32)
        s_idx = np.arange(128)[:, None]
        t_idx = np.arange(128)[None, :]
        cf[:, 0:128] = (s_idx <= t_idx).astype(np.float32)
        cf[:, 128:256] = 1.0
        start = (core % 2 == 0)
        cf[:, 256] = 0.0 if start else 1.0
        for g, w in enumerate((2, 4, 8, 16)):
            pos = np.arange(16)
            cnt = np.minimum(pos + 1, w) if start else np.full(16, w)
            cf[:, 257 + g * 16:257 + (g + 1) * 16] = (1.0 / cnt.astype(np.float32))[None, :]
        cfs.append(cf)
    return cb, cfs


def _fm(v):
    return np.ascontiguousarray(v.reshape(32, 128).T)


def kernel(x, a_w_in, a_ln_g, a_ln_b, a_w_s, a_b_s, a_w_out, b_w_in, b_w_pool, b_scale, b_w_out,
           c_w_in, c_conv_w, c_conv_b, c_ln_g, c_ln_b, c_w_out, post_ln_g, post_ln_b):
    f = lambda a: np.ascontiguousarray(np.asarray(a, dtype=np.float32))
    x = f(x)
    cb, cfs = _consts()
    post_rep = np.empty((4, 128, 2 * D), dtype=np.float32)
    post_rep[:, :, 0:D] = f(post_ln_g)[:, None, :]
    post_rep[:, :, D:] = f(post_ln_b)[:, None, :]
    shared = {"cst_bf": cb, "post_rep": post_rep}
    shared["a_w_in"] = f(a_w_in)
    shared["a_w_out"] = f(a_w_out)
    shared["a_wsT"] = np.ascontiguousarray(f(a_w_s).transpose(0, 3, 1, 2).reshape(2, 128, 1024))
    bs = f(a_b_s).reshape(2, 1, 1024)
    shared["a_bs_rep"] = np.ascontiguousarray(np.broadcast_to(bs, (2, 128, 1024)))
    av = np.empty((2, 128, 64), dtype=np.float32)
    for j in range(2):
        av[j, :, 0:32] = _fm(f(a_ln_g)[j])
        av[j, :, 32:64] = _fm(f(a_ln_b)[j])
    shared["a_vec"] = av
    shared["b_w_in"] = f(b_w_in)
    shared["b_w_pool"] = f(b_w_pool)
    shared["b_w_out"] = f(b_w_out)
    shared["b_vec"] = np.ascontiguousarray(_fm(f(b_scale)[0])[None])
    shared["c_w_in"] = f(c_w_in)
    shared["c_w_out"] = f(c_w_out)
    cv = np.empty((1, 128, 32 * 31 + 96), dtype=np.float32)
    cw = f(c_conv_w)[0]
    cv[0, :, 0:32 * 31] = cw.reshape(31, 32, 128).transpose(2, 1, 0).reshape(128, 32 * 31)
    cv[0, :, 32 * 31:32 * 31 + 32] = _fm(f(c_conv_b)[0])
    cv[0, :, 32 * 31 + 32:32 * 31 + 64] = _fm(f(c_ln_g)[0])
    cv[0, :, 32 * 31 + 64:32 * 31 + 96] = _fm(f(c_ln_b)[0])
    shared["c_vec"] = cv

    h = x.reshape(4 * 4096, D)
    for layers in LAYER_GROUPS:
        prog = _get_prog(layers)
        in_maps = []
        for core in range(NCORE):
            r0 = core * 2048
            xin = np.zeros((NTILE * 128, D), dtype=np.float32)
            xin[128:] = h[r0:r0 + 2048]
            if core % 2 == 1:
                xin[0:128] = h[r0 - 128:r0]
            m = {"x_in": xin, "cst_f": cfs[core]}
            for k in ["cst_bf", "post_rep"] + list(prog.W.keys()):
                m[k] = shared[k]
            in_maps.append(m)
        res = run_bass_kernel_spmd(prog.nc, in_maps, core_ids=list(range(NCORE)))
        h = np.concatenate([np.asarray(r["out"]) for r in res.results], axis=0)
    return h.reshape(4, 4096, D).astype(np.float32)
```
